# Optimizing a Trainium2 kernel written in Bass

```python
import jax, jax.numpy as jnp
from jax import lax
import numpy as np

D_MODEL = 1024
BATCH = 2
SEQ = 8192
DEPTH = 2

N_MIXERS = 2
N_CONV_LAYERS = (DEPTH + 1) // 2
N_HGRN_LAYERS = DEPTH // 2
CONV_WIDTH = 31
HGRN_EXPAND = 128
HGRN_HEADS = D_MODEL // HGRN_EXPAND
HGRN_KDIM = HGRN_EXPAND
HGRN_VDIM = D_MODEL // HGRN_HEADS
HGRN_FDIM = HGRN_HEADS * HGRN_KDIM
HGRN_IN_WIDTH = 2 * HGRN_FDIM + 2 * D_MODEL
CHUNK = 64
D_FF = 2816
EPS = 1e-6

kernel_name = "hybrid_conformer_conv_hgrn2_macaron"


def rms_norm(x, gain):
    x32 = x.astype(jnp.float32)
    y = x32 * lax.rsqrt(jnp.mean(x32 * x32, axis=-1, keepdims=True) + EPS)
    return (y * gain.astype(jnp.float32)).astype(x.dtype)


def layer_norm(x, gain, bias):
    x32 = x.astype(jnp.float32)
    mu = jnp.mean(x32, axis=-1, keepdims=True)
    xc = x32 - mu
    y = xc * lax.rsqrt(jnp.mean(xc * xc, axis=-1, keepdims=True) + EPS)
    return (y * gain.astype(jnp.float32) + bias.astype(jnp.float32)).astype(x.dtype)


def swiglu_ffn(x, w_in, w_out):
    gate, up = jnp.split(x @ w_in, 2, axis=-1)
    return (jax.nn.silu(gate) * up) @ w_out


def conformer_conv_module(x, w_in, b_in, w_dw, b_dw, ln_g, ln_b, w_out, b_out):
    a, gate = jnp.split(x @ w_in + b_in, 2, axis=-1)
    h = a * jax.nn.sigmoid(gate)
    h = lax.conv_general_dilated(
        h, w_dw[:, None, :].astype(h.dtype),
        window_strides=(1,), padding=[(CONV_WIDTH - 1, 0)],
        dimension_numbers=("NWC", "WIO", "NWC"),
        feature_group_count=D_MODEL) + b_dw
    h = jax.nn.silu(layer_norm(h, ln_g, ln_b))
    return h @ w_out + b_out


def hgrn2_mixer(x, w_in, lower_bound, g_norm_w, w_out):
    bsz, seq, _ = x.shape
    n_chunks = seq // CHUNK
    q, f, i, g = jnp.split(x @ w_in, [HGRN_FDIM, 2 * HGRN_FDIM, 2 * HGRN_FDIM + D_MODEL], axis=-1)
    q = jax.nn.silu(q.astype(jnp.float32))
    f = f.astype(jnp.float32)
    lb = lower_bound.astype(jnp.float32)
    log_f = jnp.logaddexp(jnp.log(lb), jnp.log1p(-lb) + jax.nn.log_sigmoid(f))
    k = (1.0 - lb) * jax.nn.sigmoid(-f)
    v = i.astype(jnp.float32)

    def to_chunks(t, d):
        return t.reshape(bsz, n_chunks, CHUNK, HGRN_HEADS, d).transpose(1, 0, 3, 2, 4)

    qc, kc, gc = to_chunks(q, HGRN_KDIM), to_chunks(k, HGRN_KDIM), to_chunks(log_f, HGRN_KDIM)
    vc = to_chunks(v, HGRN_VDIM)
    causal = jnp.tril(jnp.ones((CHUNK, CHUNK), dtype=bool))[None, None, :, :, None]

    def step(state, inp):
        qb, kb, vb, gb = inp
        b = jnp.cumsum(gb, axis=2)
        o_inter = jnp.einsum("bhtk,bhkv->bhtv", qb * jnp.exp(b), state)
        diff = b[:, :, :, None, :] - b[:, :, None, :, :]
        decay = jnp.exp(jnp.where(causal, diff, -jnp.inf))
        scores = jnp.einsum("bhtk,bhsk,bhtsk->bhts", qb, kb, decay)
        o_intra = jnp.einsum("bhts,bhsv->bhtv", scores, vb)
        b_last = b[:, :, -1]
        k_dec = kb * jnp.exp(b_last[:, :, None, :] - b)
        new_state = jnp.exp(b_last)[..., None] * state + jnp.einsum("bhsk,bhsv->bhkv", k_dec, vb)
        return new_state, o_inter + o_intra

    state0 = jnp.zeros((bsz, HGRN_HEADS, HGRN_KDIM, HGRN_VDIM), jnp.float32)
    _, o = lax.scan(step, state0, (qc, kc, vc, gc))
    o = o.transpose(1, 0, 3, 2, 4).reshape(bsz, seq, HGRN_HEADS, HGRN_VDIM)
    o = rms_norm(o, g_norm_w) * jax.nn.silu(g.astype(jnp.float32)).reshape(bsz, seq, HGRN_HEADS, HGRN_VDIM)
    return o.reshape(bsz, seq, D_MODEL).astype(x.dtype) @ w_out


def setup_inputs(seed: int = 0) -> dict:
    key = jax.random.key(seed)
    ks = jax.random.split(key, 20)
    nrm = lambda k, shape, s: jax.random.normal(k, shape, jnp.float32) * s
    D = D_MODEL
    return {
        "x": nrm(ks[0], (BATCH, SEQ, D), 1.0),
        "norm_gains": 1.0 + nrm(ks[1], (DEPTH, 6, D), 0.02),
        "ffn_w_in": nrm(ks[2], (DEPTH, 2, D, 2 * D_FF), D ** -0.5),
        "ffn_w_out": nrm(ks[3], (DEPTH, 2, D_FF, D), D_FF ** -0.5),
        "conv_w_in": nrm(ks[4], (N_CONV_LAYERS, D, 2 * D), D ** -0.5),
        "conv_b_in": nrm(ks[5], (N_CONV_LAYERS, 2 * D), 0.01),
        "conv_w_dw": nrm(ks[6], (N_CONV_LAYERS, CONV_WIDTH, D), CONV_WIDTH ** -0.5),
        "conv_b_dw": nrm(ks[7], (N_CONV_LAYERS, D), 0.01),
        "conv_ln_g": 1.0 + nrm(ks[8], (N_CONV_LAYERS, D), 0.02),
        "conv_ln_b": nrm(ks[9], (N_CONV_LAYERS, D), 0.01),
        "conv_w_out": nrm(ks[10], (N_CONV_LAYERS, D, D), D ** -0.5),
        "conv_b_out": nrm(ks[11], (N_CONV_LAYERS, D), 0.01),
        "hgrn_w_in": nrm(ks[12], (N_HGRN_LAYERS, D, HGRN_IN_WIDTH), D ** -0.5),
        "hgrn_lb_logits": nrm(ks[13], (DEPTH, HGRN_FDIM), 0.1),
        "hgrn_g_norm": 1.0 + nrm(ks[14], (N_HGRN_LAYERS, HGRN_VDIM), 0.02),
        "hgrn_w_out": nrm(ks[15], (N_HGRN_LAYERS, D, D), D ** -0.5),
    }


def reference(x, norm_gains, ffn_w_in, ffn_w_out, conv_w_in, conv_b_in, conv_w_dw, conv_b_dw,
              conv_ln_g, conv_ln_b, conv_w_out, conv_b_out, hgrn_w_in, hgrn_lb_logits,
              hgrn_g_norm, hgrn_w_out):
    p = jax.nn.softmax(hgrn_lb_logits.astype(jnp.float32), axis=0)
    lower_bounds = jnp.cumsum(p, axis=0) - p[0:1]

    for layer in range(DEPTH):
        g = norm_gains[layer]
        h = swiglu_ffn(rms_norm(x, g[0]), ffn_w_in[layer, 0], ffn_w_out[layer, 0])
        x = x + 0.5 * rms_norm(h, g[1])
        hn = rms_norm(x, g[2])
        j = layer // N_MIXERS
        if layer % N_MIXERS == 0:
            m = conformer_conv_module(hn, conv_w_in[j], conv_b_in[j], conv_w_dw[j], conv_b_dw[j],
                                      conv_ln_g[j], conv_ln_b[j], conv_w_out[j], conv_b_out[j])
        else:
            m = hgrn2_mixer(hn, hgrn_w_in[j], lower_bounds[layer], hgrn_g_norm[j], hgrn_w_out[j])
        x = x + rms_norm(m, g[3])
        h = swiglu_ffn(rms_norm(x, g[4]), ffn_w_in[layer, 1], ffn_w_out[layer, 1])
        x = x + 0.5 * rms_norm(h, g[5])
    return x
```

```python
import numpy as np
from contextlib import ExitStack
import concourse.bass as bass
import concourse.mybir as mybir
from concourse.bass_utils import run_bass_kernel_spmd

F32 = mybir.dt.float32
BF16 = mybir.dt.bfloat16
AF = mybir.ActivationFunctionType
ALU = mybir.AluOpType

D = 1024
KC = 8
DFF = 2816
JC = 22
G = 1024
TT = 512
NT = G // TT
SEQ = 8192
CW = 31
EPS = 1e-6
NCORES = 2
NIO = 2
SAME_SYNC = True
DBG_STAGES = None
DBG_SUB = 99
DBGX = ''

OFF_G, OFF_CBIN, OFF_CBDW, OFF_LNG, OFF_LNB, OFF_CBO, OFF_WDW, OFF_LBL, OFF_GN = 0, 96, 112, 120, 128, 136, 144, 392, 408
NPV = 512


class Res:
    __slots__ = ("w", "rs", "x")

    def __init__(self, x=False):
        self.w = None
        self.rs = []
        self.x = x


class Buf:
    def __init__(self, t, init=None):
        self.t = t
        self.res = {}
        self.init = init or []

    def R(self, *key):
        r = self.res.get(key)
        if r is None:
            r = self.res[key] = Res()
            r.rs = list(self.init)
        return r


class Slot:
    def __init__(self, t, sem):
        self.t = t
        self.sem = sem
        self.cnt = 0
        self.res = Res()


class Eng:
    def __init__(self, obj, sem):
        self.obj = obj
        self.sem = sem
        self.count = 0
        self.seen = {}


class KB:
    def __init__(self, nc, es):
        self.nc = nc
        self.es = es
        self.eng = {}
        self.nsem = 0

    def sem(self, name):
        self.nsem += 1
        return self.es.enter_context(self.nc.semaphore(name))

    def add_engine(self, name, obj):
        self.eng[name] = Eng(obj, self.sem("e_" + name))

    def _wait(self, e, r, w):
        strong, weak, sems = {}, {}, {}

        def add(d, tok):
            if tok is None:
                return
            k = tok[0].name
            sems[k] = tok[0]
            if d.get(k, 0) < tok[1]:
                d[k] = tok[1]

        for x in r:
            add(strong, x.w)
            if x.x:
                for t in x.rs:
                    add(weak, t)
        for x in w:
            add(strong, x.w)
            for t in x.rs:
                add(weak, t)
        for k, sem in sems.items():
            need = strong.get(k, 0)
            if SAME_SYNC or sem is not e.sem:
                need = max(need, weak.get(k, 0))
            if need > 0 and e.seen.get(k, 0) < need:
                e.obj.wait_ge(sem, need)
                e.seen[k] = need

    @staticmethod
    def _mark(tok, r, w):
        for x in r:
            x.rs.append(tok)
        for x in w:
            x.w = tok
            x.rs = []

    def op(self, en, fn, r=(), w=()):
        e = self.eng[en]
        self._wait(e, r, w)
        ins = fn(e.obj)
        e.count += 1
        ins.then_inc(e.sem, 1)
        self._mark((e.sem, e.count), r, w)

    def grp(self, en, fns, r=(), w=()):
        e = self.eng[en]
        self._wait(e, r, w)
        ins = None
        for fn in fns:
            ins = fn(e.obj)
        e.count += 1
        ins.then_inc(e.sem, 1)
        self._mark((e.sem, e.count), r, w)

    def dma(self, q, fns, slot, r=(), w=()):
        e = self.eng[q]
        self._wait(e, r, w)
        for fn in fns:
            fn(e.obj).then_inc(slot.sem, 16)
            slot.cnt += 16
        self._mark((slot.sem, slot.cnt), r, w)

    def barrier(self, names=("pe", "act", "dve", "pool")):
        for a in names:
            ea = self.eng[a]
            for b in names:
                if a == b:
                    continue
                eb = self.eng[b]
                if eb.count > 0 and ea.seen.get(eb.sem.name, 0) < eb.count:
                    ea.obj.wait_ge(eb.sem, eb.count)
                    ea.seen[eb.sem.name] = eb.count


class WStream:
    def __init__(self, kb, slots, plan, lookahead, issue_fn):
        self.kb, self.slots, self.plan, self.la, self.issue_fn = kb, slots, plan, lookahead, issue_fn
        self.issued = 0
        self.cur = 0

    def _issue_upto(self, lim):
        lim = min(len(self.plan), lim)
        while self.issued < lim:
            self.issue_fn(self.slots[self.issued % len(self.slots)], self.plan[self.issued])
            self.issued += 1

    def next(self, la=None):
        self._issue_upto(self.cur + (self.la if la is None else la) + 1)
        s = self.slots[self.cur % len(self.slots)]
        self.cur += 1
        return s

    def prefetch(self, idx):
        self._issue_upto(idx + 1)


class Rot:
    def __init__(self, items):
        self.items = items
        self.i = 0

    def next(self):
        x = self.items[self.i % len(self.items)]
        self.i += 1
        return x


def build_program(seq_t=SEQ):
    NG = seq_t // G
    nc = bass.Bass("TRN2", target_bir_lowering=False)
    dt = lambda name, shape, kind: nc.dram_tensor(name, shape, F32, kind=kind).ap()
    x_d = dt("x", [seq_t, D], "ExternalInput")
    fwi_d = dt("ffn_w_in", [2, 2, D, 2 * DFF], "ExternalInput")
    fwo_d = dt("ffn_w_out", [2, 2, DFF, D], "ExternalInput")
    cwi_d = dt("conv_w_in", [D, 2 * D], "ExternalInput")
    cwo_d = dt("conv_w_out", [D, D], "ExternalInput")
    hwi_d = dt("hgrn_w_in", [D, 4 * D], "ExternalInput")
    hwo_d = dt("hgrn_w_out", [D, D], "ExternalInput")
    pv_d = dt("pvec", [NPV, 128], "ExternalInput")
    id_d = dt("ident", [128, 128], "ExternalInput")
    cm_d = dt("cmask", [128, 128], "ExternalInput")
    y_d = dt("y", [seq_t, D], "ExternalOutput")

    with ExitStack() as es:
        kb = KB(nc, es)
        kb.add_engine("pe", nc.tensor)
        kb.add_engine("act", nc.scalar)
        kb.add_engine("dve", nc.vector)
        kb.add_engine("pool", nc.gpsimd)
        kb.add_engine("sp", nc.sync)

        def sb(name, shape, dtype=F32):
            return es.enter_context(nc.sbuf_tensor("s_" + name, shape, dtype))

        x = Buf(sb("x", [128, KC, G]))
        xn = Buf(sb("xn", [128, KC, G], BF16))
        y = Buf(sb("y", [128, KC, G]))
        sqb = [Buf(sb(f"sq{i}", [128, KC, TT], BF16)) for i in range(2)]
        wA = [Slot(sb(f"wA{i}", [128, 4, KC, 128], BF16), kb.sem(f"wA{i}")) for i in range(3)]
        wO = [Slot(sb(f"wO{i}", [128, JC, 128], BF16), kb.sem(f"wO{i}")) for i in range(2)]
        S = Buf(sb("S", [128, KC, 128]))
        halo = Buf(sb("halo", [128, KC, 32], BF16))
        pv = sb("pv", [128, NPV])
        pvh = sb("pvh", [128, 96])
        lbv = sb("lbv", [128, 8])
        oml = sb("oml", [128, 8])
        hl = sb("hl", [128, 8])
        b1 = sb("b1", [128, 8])
        c1 = sb("c1", [128, 8])
        ident = sb("ident", [128, 128])
        identb = sb("identb", [128, 128], BF16)
        cmask = sb("cmask", [128, 128])
        ones_d = sb("ones_d", [128, 128], BF16)
        ones_h = sb("ones_h", [128, 128], BF16)
        onesf = sb("onesf", [128, TT])
        mhalf1 = sb("mhalf1", [128, 1])
        epsb = sb("epsb", [128, 1])
        stf = Rot([Buf(sb(f"stf{i}", [128, TT])) for i in range(2)])
        rsf = Rot([Buf(sb(f"rsf{i}", [128, TT])) for i in range(2)])
        io = [Slot(sb(f"io{i}", [128, D]), kb.sem(f"io{i}")) for i in range(NIO)]
        ioR = Rot(io)
        dec = Buf(sb("dec", [128, KC, 4]))
        ssc = Buf(sb("ssc", [128, KC, 4]))
        sm = Rot([Buf(sb(f"sm{i}", [128, 8])) for i in range(4)])
        banks = [(es.enter_context(nc.psum_tensor(f"psum{i}", [128, TT], F32)), Res(True)) for i in range(8)]
        bank_rot = Rot(banks)

        def bank():
            b = bank_rot.next()
            assert b[1].w is None or len(b[1].rs) > 0, "PSUM bank re-allocated before its consumer was emitted"
            return b

        c_res = Res()
        pslot = Slot(None, kb.sem("pload"))
        kb.dma("sp", [lambda e: e.dma_start(out=ident[:], in_=id_d[:, :]),
                      lambda e: e.dma_start(out=cmask[:], in_=cm_d[:, :])], pslot, w=[c_res])
        kb.dma("sp", [lambda e, i=i: e.dma_start(out=io[0].t[:, i * 128:(i + 1) * 128], in_=pv_d[i * 128:(i + 1) * 128, :]) for i in range(4)],
               io[0], w=[io[0].res])
        ps, pr = bank()
        kb.grp("pe", [lambda e, i=i: e.transpose(ps[:, i * 128:(i + 1) * 128], io[0].t[:, i * 128:(i + 1) * 128], ident[:]) for i in range(4)],
               r=[c_res, io[0].res], w=[pr])
        pv_res = Res()
        kb.op("dve", lambda e: e.tensor_copy(out=pv[:], in_=ps[:]), r=[pr], w=[pv_res])
        kb.op("dve", lambda e: e.tensor_scalar(out=pvh[:], in0=pv[:, 0:96], scalar1=0.5, scalar2=None, op0=ALU.mult),
              r=[pv_res], w=[pv_res])
        kb.op("dve", lambda e: e.tensor_tensor(out=lbv[:], in0=pv[:, OFF_LBL + 8:OFF_LBL + 16], in1=pv[:, OFF_LBL:OFF_LBL + 8],
                                               op=ALU.subtract), r=[pv_res], w=[pv_res])
        kb.op("act", lambda e: e.activation(out=lbv[:], in_=lbv[:], func=AF.Sigmoid), r=[pv_res], w=[pv_res])
        kb.op("dve", lambda e: e.tensor_scalar(out=oml[:], in0=lbv[:], scalar1=-1.0, scalar2=1.0, op0=ALU.mult, op1=ALU.add),
              r=[pv_res], w=[pv_res])
        kb.op("dve", lambda e: e.tensor_scalar(out=hl[:], in0=oml[:], scalar1=0.5, scalar2=None, op0=ALU.mult), r=[pv_res], w=[pv_res])
        kb.op("dve", lambda e: e.tensor_tensor(out=b1[:], in0=hl[:], in1=lbv[:], op=ALU.add), r=[pv_res], w=[pv_res])
        kb.op("dve", lambda e: e.tensor_scalar(out=c1[:], in0=hl[:], scalar1=-1.0, scalar2=None, op0=ALU.mult), r=[pv_res], w=[pv_res])
        kb.op("dve", lambda e: e.tensor_copy(out=identb[:], in_=ident[:]), r=[c_res], w=[c_res])
        kb.op("dve", lambda e: e.memset(ones_d[:], 1.0 / 1024.0), w=[c_res])
        kb.op("dve", lambda e: e.memset(ones_h[:], 1.0 / 128.0), w=[c_res])
        kb.op("dve", lambda e: e.memset(onesf[:], 1.0), w=[c_res])
        kb.op("dve", lambda e: e.memset(mhalf1[:], -0.5), w=[c_res])
        kb.op("dve", lambda e: e.memset(epsb[:], EPS), w=[c_res])
        mhalf = mhalf1[:].to_broadcast([128, TT])
        kb.op("dve", lambda e: e.memset(S.t[:], 0.0), w=[S.R(hh) for hh in range(KC)])
        kb.op("dve", lambda e: e.memset(halo.t[:], 0.0), w=[halo.R(0)])
        kb.eng["sp"].obj.wait_ge(pslot.sem, pslot.cnt)
        kb.barrier(("pe", "act", "dve", "pool"))
        for en in ("pe", "act", "pool"):
            e = kb.eng[en]
            e.obj.wait_ge(kb.eng["dve"].sem, kb.eng["dve"].count)
            e.seen[kb.eng["dve"].sem.name] = kb.eng["dve"].count

        def pcol(i):
            return pv[:, i:i + 1]

        planA, planO = [], []

        def issueA(slot, spec):
            fns = []
            for (sub, w2d, col0) in spec:
                src = w2d.rearrange("(kc p) n -> p kc n", p=128)[:, :, col0:col0 + 128]
                fns.append(lambda e, sub=sub, src=src: e.dma_start(out=slot.t[:, sub, :, :], in_=src))
            kb.dma("pool", fns, slot, w=[slot.res])

        def issueO(slot, spec):
            w2d, col0 = spec
            src = w2d.rearrange("(j p) n -> p j n", p=128)[:, :, col0:col0 + 128]
            kb.dma("pool", [lambda e: e.dma_start(out=slot.t[:], in_=src)], slot, w=[slot.res])

        class Ctx:
            plan = True
        ctx = Ctx()

        def tsl(t):
            return slice(t * TT, (t + 1) * TT)

        def rstd_from_sq(sq, onesmat, eps=EPS, dst=None):
            ps, pr = bank()
            kb.grp("pe", [lambda e, c=c: e.matmul(ps[:], lhsT=onesmat[:], rhs=sq.t[:, c, :], start=(c == 0), stop=(c == KC - 1))
                          for c in range(KC)], r=[sq.R(c) for c in range(KC)], w=[pr])
            t = rsf.next() if dst is None else dst
            kb.op("act", lambda e: e.activation(out=t.t[:], in_=ps[:], func=AF.Ln, bias=epsb[:, 0:1]), r=[pr], w=[t.R(0)])
            kb.op("act", lambda e: e.activation(out=t.t[:], in_=t.t[:], func=AF.Exp, scale=-0.5), r=[t.R(0)], w=[t.R(0)])
            return t

        PRE = {"stats": False}

        def pre_stats(t):
            sq = sqb[t % 2]
            kb.op("act", lambda e: e.activation(out=sq.t[:], in_=x.t[:, :, tsl(t)], func=AF.Square),
                  r=[x.R(c, t) for c in range(KC)], w=[sq.R(c) for c in range(KC)])
            rstd_from_sq(sq, ones_d, dst=stf.items[t])

        def pre_apply(gcol, t):
            rs = stf.items[t]
            for c in range(KC):
                kb.op("dve", lambda e, c=c: e.scalar_tensor_tensor(out=xn.t[:, c, tsl(t)], in0=x.t[:, c, tsl(t)],
                                                                    scalar=pcol(gcol + c), in1=rs.t[:], op0=ALU.mult, op1=ALU.mult),
                      r=[x.R(c, t), rs.R(0)], w=[xn.R(c, t)])

        def prenorm(gcol):
            for t in range(NT):
                if not PRE["stats"]:
                    pre_stats(t)
                pre_apply(gcol, t)
            PRE["stats"] = False

        def postnorm_t(gsrc, gcol, t):
            rs = rstd_from_sq(sqb[t % 2], ones_d)
            for c in range(KC):
                kb.op("dve", lambda e, c=c: e.scalar_tensor_tensor(out=y.t[:, c, tsl(t)], in0=y.t[:, c, tsl(t)],
                                                                    scalar=gsrc[:, gcol + c:gcol + c + 1], in1=rs.t[:],
                                                                    op0=ALU.mult, op1=ALU.mult),
                      r=[y.R(c, t), rs.R(0)], w=[y.R(c, t)])
                kb.op("dve", lambda e, c=c: e.tensor_tensor(out=x.t[:, c, tsl(t)], in0=x.t[:, c, tsl(t)], in1=y.t[:, c, tsl(t)],
                                                            op=ALU.add), r=[y.R(c, t), x.R(c, t)], w=[x.R(c, t)])

        def out_proj(stream, lhs_fn, nk, rhs_fn, rhs_res_fn, bias_col, post, tile_outer=False):
            def one(slot, m, t):
                ps, pr = bank()
                kb.grp("pe", [lambda e, k=k: e.matmul(ps[:], lhsT=lhs_fn(slot, k), rhs=rhs_fn(k, t), start=(k == 0), stop=(k == nk - 1))
                              for k in range(nk)], r=[slot.res] + rhs_res_fn(t), w=[pr])
                if bias_col is None:
                    kb.op("act", lambda e: e.activation(out=y.t[:, m, tsl(t)], in_=ps[:], func=AF.Copy), r=[pr], w=[y.R(m, t)])
                    kb.op("act", lambda e: e.activation(out=sqb[t % 2].t[:, m, :], in_=ps[:], func=AF.Square),
                          r=[pr], w=[sqb[t % 2].R(m)])
                else:
                    kb.op("act", lambda e: e.activation(out=y.t[:, m, tsl(t)], in_=ps[:], func=AF.Identity,
                                                        bias=pcol(bias_col + m)), r=[pr], w=[y.R(m, t)])
                    kb.op("act", lambda e: e.activation(out=sqb[t % 2].t[:, m, :], in_=ps[:], func=AF.Square,
                                                        bias=pcol(bias_col + m)), r=[pr], w=[sqb[t % 2].R(m)])

            if tile_outer:
                for t in range(NT):
                    for m in range(KC):
                        one(stream.next(), m, t)
                    post(t)
                return
            for m in range(KC - 2):
                slot = stream.next()
                for t in range(NT):
                    one(slot, m, t)
            base = stream.cur
            sa = stream.next()
            sb_ = stream.next(la=stream.la - 1)
            one(sa, KC - 2, 0)
            one(sb_, KC - 1, 0)
            one(sa, KC - 2, 1)
            stream.prefetch(base + 1 + stream.la)
            post(0)
            one(sb_, KC - 1, 1)
            post(1)

        class Stg:
            def __init__(self, ap, rl, slot):
                self.ap, self.rl, self.slot = ap, rl, slot

        ystg = [Stg(y.t[:, c, :], [y.R(c, 0), y.R(c, 1)], Slot(None, kb.sem(f"ys{c}"))) for c in range(KC)]
        stg_store = Rot(ystg[0:4] + [Stg(io[i].t[:], [io[i].res], io[i]) for i in range(NIO)])
        stg_load = Rot(ystg[4:8])

        def load_group(g, tbs=range(G // 128)):
            if ctx.plan:
                return
            for tb in tbs:
                st = stg_load.next()
                r0 = g * G + tb * 128
                kb.dma("sp", [lambda e: e.dma_start(out=st.ap, in_=x_d[r0:r0 + 128, :])], st.slot, w=st.rl)
                for half in range(2):
                    ps, pr = bank()
                    kb.grp("pe", [lambda e, i=i: e.transpose(ps[:, i * 128:(i + 1) * 128],
                                                             st.ap[:, (half * 4 + i) * 128:(half * 4 + i + 1) * 128], ident[:])
                                  for i in range(4)], r=st.rl, w=[pr])
                    dst = x.t[:, half * 4:half * 4 + 4, tb * 128:(tb + 1) * 128]
                    src = ps[:].rearrange("p (c t) -> p c t", t=128)
                    wr = [x.R(half * 4 + i, tb // 4) for i in range(4)]
                    if half == 0:
                        kb.op("act", lambda e: e.activation(out=dst, in_=src, func=AF.Copy), r=[pr], w=wr)
                    else:
                        kb.op("dve", lambda e: e.tensor_copy(out=dst, in_=src), r=[pr], w=wr)

        def store_group(g, tbs=range(G // 128)):
            if ctx.plan:
                return
            for tb in tbs:
                st = stg_store.next()
                for half in range(2):
                    ps, pr = bank()
                    kb.grp("pe", [lambda e, i=i: e.transpose(ps[:, i * 128:(i + 1) * 128],
                                                             x.t[:, half * 4 + i, tb * 128:(tb + 1) * 128], ident[:])
                                  for i in range(4)], r=[x.R(half * 4 + i, tb // 4) for i in range(4)], w=[pr])
                    dst = st.ap[:, half * 512:(half + 1) * 512]
                    if half == 0:
                        kb.op("act", lambda e: e.activation(out=dst, in_=ps[:], func=AF.Copy), r=[pr], w=st.rl)
                    else:
                        kb.op("dve", lambda e: e.tensor_copy(out=dst, in_=ps[:]), r=[pr], w=st.rl)
                r0 = g * G + tb * 128
                kb.dma("sp", [lambda e: e.dma_start(out=y_d[r0:r0 + 128, :], in_=st.ap)], st.slot, r=st.rl)

        def ffn(l, i, h):
            win = fwi_d[l, i]
            wout = fwo_d[l, i]
            sA = [[(0, win, j * 128), (1, win, DFF + j * 128)] for j in range(JC)]
            sO = [(wout, m * 128) for m in range(KC)]
            if ctx.plan:
                planA.extend(sA)
                planO.extend(sO)
                return
            prenorm(OFF_G + (l * 6 + (0 if i == 0 else 4)) * 8)
            def hidden(slot, j, t):
                pg, prg = bank()
                kb.grp("pe", [lambda e, k=k: e.matmul(pg[:], lhsT=slot.t[:, 0, k, :], rhs=xn.t[:, k, tsl(t)],
                                                      start=(k == 0), stop=(k == KC - 1)) for k in range(KC)],
                       r=[slot.res] + [xn.R(k, t) for k in range(KC)], w=[prg])
                pu, pru = bank()
                kb.grp("pe", [lambda e, k=k: e.matmul(pu[:], lhsT=slot.t[:, 1, k, :], rhs=xn.t[:, k, tsl(t)],
                                                      start=(k == 0), stop=(k == KC - 1)) for k in range(KC)],
                       r=[slot.res] + [xn.R(k, t) for k in range(KC)], w=[pru])
                st = stf.next()
                kb.op("act", lambda e: e.activation(out=st.t[:], in_=pg[:], func=AF.Silu), r=[prg], w=[st.R(0)])
                kb.op("dve", lambda e: e.tensor_tensor(out=h.t[:, j, tsl(t)], in0=st.t[:], in1=pu[:], op=ALU.mult),
                      r=[st.R(0), pru], w=[h.R(j, t)])

            HEADSTART = False
            if HEADSTART:
                base = strA.cur
                s3 = [strA.next(la=2 - i) for i in range(3)]
                for i in range(3):
                    hidden(s3[i], i, 0)
                for i in range(3):
                    hidden(s3[i], i, 1)
                    strA.prefetch(base + 3 + i)
            for j in range(3 if HEADSTART else 0, JC):
                slot = strA.next()
                for t in range(NT):
                    hidden(slot, j, t)

            last = (l == 1 and i == 1)

            def post(t):
                postnorm_t(pvh, (l * 6 + (1 if i == 0 else 5)) * 8, t)
                if not last and t == NT - 1:
                    for tt in range(NT):
                        pre_stats(tt)
                    PRE["stats"] = True
            out_proj(strO, lambda slot, k: slot.t[:, k, :], JC, lambda k, t: h.t[:, k, tsl(t)],
                     lambda t: [h.R(k, t) for k in range(JC)], None,
                     post)

        def conv_mixer(hc, diag, hb):
            sA_in = [[(0, cwi_d, c * 128), (1, cwi_d, D + c * 128)] for c in range(KC)]
            sA_out = [[(0, cwo_d, m * 128)] for m in range(KC)] * NT
            if ctx.plan:
                planA.extend(sA_in)
                planA.extend(sA_out)
                return
            prenorm(OFF_G + 2 * 8)
            kb.op("dve", lambda e: e.tensor_copy(out=hc.t[:, :, 0:32], in_=halo.t[:]), r=[halo.R(0)],
                  w=[hc.R(c, -1) for c in range(KC)])
            if DBG_SUB <= -2:
                return
            for c in range(KC):
                slot = strA.next()
                for t in range(NT):
                    pa, pra = bank()
                    kb.grp("pe", [lambda e, k=k: e.matmul(pa[:], lhsT=slot.t[:, 0, k, :], rhs=xn.t[:, k, tsl(t)],
                                                          start=(k == 0), stop=(k == KC - 1)) for k in range(KC)],
                           r=[slot.res] + [xn.R(k, t) for k in range(KC)], w=[pra])
                    pg, prg = bank()
                    kb.grp("pe", [lambda e, k=k: e.matmul(pg[:], lhsT=slot.t[:, 1, k, :], rhs=xn.t[:, k, tsl(t)],
                                                          start=(k == 0), stop=(k == KC - 1)) for k in range(KC)],
                           r=[slot.res] + [xn.R(k, t) for k in range(KC)], w=[prg])
                    st = stf.next()
                    kb.op("act", lambda e: e.activation(out=st.t[:], in_=pg[:], func=AF.Sigmoid, bias=pcol(OFF_CBIN + 8 + c)),
                          r=[prg], w=[st.R(0)])
                    kb.op("dve", lambda e: e.scalar_tensor_tensor(out=hc.t[:, c, 32 + t * TT:32 + (t + 1) * TT], in0=pa[:],
                                                                  scalar=pcol(OFF_CBIN + c), in1=st.t[:], op0=ALU.add, op1=ALU.mult),
                          r=[pra, st.R(0)], w=[hc.R(c, t)])
            if DBG_SUB <= -1:
                return
            kb.op("dve", lambda e: e.tensor_copy(out=halo.t[:], in_=hc.t[:, :, G:G + 32]),
                  r=[hc.R(c, NT - 1) for c in range(KC)], w=[halo.R(0)])
            if DBG_SUB <= 1:
                return
            def build_diag(c):
                dg = diag[c % 2]
                wsl = pv[:, OFF_WDW + c:OFF_WDW + c + 8 * CW:8]
                kb.op("dve", lambda e: e.tensor_tensor(out=dg.t[:], in0=identb[:].unsqueeze(1).to_broadcast([128, CW, 128]),
                                                       in1=wsl.unsqueeze(2).to_broadcast([128, CW, 128]), op=ALU.mult),
                      w=[dg.R(0)])

            build_diag(0)
            for c in range(KC):
                dg = diag[c % 2]
                if c + 1 < KC:
                    build_diag(c + 1)
                for t in range(NT):
                    ps, pr = bank()
                    rr = [dg.R(0), hc.R(c, t), hc.R(c, t - 1)]
                    kb.grp("pe", [lambda e, k=k: e.matmul(ps[:], lhsT=(identb[:] if 'A' in DBGX else dg.t[:, k, :]),
                                                          rhs=(xn.t[:, c, tsl(t)] if 'D' in DBGX else hc.t[:, c, 32 + t * TT + k - 30:32 + t * TT + k - 30 + TT]),
                                                          start=(k == 0), stop=(k == (7 if 'E' in DBGX else CW - 1))) for k in range(8 if 'E' in DBGX else CW)], r=rr, w=[pr])
                    bcol = pcol(OFF_CBDW + c)
                    if 'B' in DBGX:
                        kb.op("act", lambda e: e.activation(out=y.t[:, c, tsl(t)], in_=ps[:], func=AF.Copy), r=[pr], w=[y.R(c, t)])
                        continue
                    if 'H' not in DBGX and 'I' not in DBGX:
                        kb.op("act", lambda e: e.activation(out=y.t[:, c, tsl(t)], in_=ps[:], func=AF.Identity, bias=bcol),
                              r=[pr], w=[y.R(c, t)])
                    if 'G' not in DBGX and 'I' not in DBGX:
                        kb.op("act", lambda e: e.activation(out=sqb[t % 2].t[:, c, :], in_=ps[:], func=AF.Square, bias=bcol),
                              r=[pr], w=[sqb[t % 2].R(c)])
                    if 'G' not in DBGX and 'H' not in DBGX and 'J' not in DBGX:
                        kb.op("dve", lambda e: e.tensor_scalar(out=xn.t[:, c, tsl(t)], in0=ps[:], scalar1=bcol, scalar2=None, op0=ALU.add),
                              r=[pr], w=[xn.R(c, t)])
            if DBG_SUB <= 2:
                return
            for t in range(NT):
                p1, pr1 = bank()
                kb.grp("pe", [lambda e, c=c: e.matmul(p1[:], lhsT=ones_d[:], rhs=xn.t[:, c, tsl(t)], start=(c == 0), stop=(c == KC - 1))
                              for c in range(KC)], r=[xn.R(c, t) for c in range(KC)], w=[pr1])
                p2, pr2 = bank()
                kb.grp("pe", [lambda e, c=c: e.matmul(p2[:], lhsT=ones_d[:], rhs=sqb[t % 2].t[:, c, :], start=(c == 0), stop=(c == KC - 1))
                              for c in range(KC)], r=[sqb[t % 2].R(c) for c in range(KC)], w=[pr2])
                m1 = stf.next()
                kb.op("dve", lambda e: e.tensor_copy(out=m1.t[:], in_=p1[:]), r=[pr1], w=[m1.R(0)])
                rs = rsf.next()
                kb.op("dve", lambda e: e.tensor_tensor(out=rs.t[:], in0=m1.t[:], in1=m1.t[:], op=ALU.mult), r=[m1.R(0)], w=[rs.R(0)])
                kb.op("dve", lambda e: e.scalar_tensor_tensor(out=rs.t[:], in0=p2[:], scalar=EPS, in1=rs.t[:], op0=ALU.add,
                                                              op1=ALU.subtract), r=[pr2, rs.R(0)], w=[rs.R(0)])
                kb.op("act", lambda e: e.activation(out=rs.t[:], in_=rs.t[:], func=AF.Ln), r=[rs.R(0)], w=[rs.R(0)])
                kb.op("act", lambda e: e.activation(out=rs.t[:], in_=rs.t[:], func=AF.Exp, scale=-0.5), r=[rs.R(0)], w=[rs.R(0)])
                for c in range(KC):
                    kb.op("dve", lambda e, c=c: e.tensor_tensor(out=y.t[:, c, tsl(t)], in0=y.t[:, c, tsl(t)], in1=m1.t[:],
                                                                op=ALU.subtract), r=[y.R(c, t), m1.R(0)], w=[y.R(c, t)])
                    kb.op("dve", lambda e, c=c: e.tensor_tensor(out=y.t[:, c, tsl(t)], in0=y.t[:, c, tsl(t)], in1=rs.t[:],
                                                                 op=ALU.mult), r=[y.R(c, t), rs.R(0)], w=[y.R(c, t)])
                    kb.op("act", lambda e, c=c: e.activation(out=xn.t[:, c, tsl(t)], in_=y.t[:, c, tsl(t)], func=AF.Silu,
                                                             bias=pcol(OFF_LNB + c), scale=pcol(OFF_LNG + c)),
                          r=[y.R(c, t)], w=[xn.R(c, t)])

            def postc(t):
                postnorm_t(pv, OFF_G + 3 * 8, t)
                if t == NT - 1:
                    for tt in range(NT):
                        pre_stats(tt)
                    PRE["stats"] = True
            out_proj(strA, lambda slot, k: slot.t[:, 0, k, :], KC, lambda k, t: xn.t[:, k, tsl(t)],
                     lambda t: [xn.R(k, t) for k in range(KC)], OFF_CBO, postc, tile_outer=True)

        def hgrn_mixer(qt, kt, sg, kd, vt, scT, Sbf, osq, onf):
            sA_in = []
            for t in range(NT):
                for hd in range(KC):
                    sA_in.append([(0, hwi_d, hd * 128), (1, hwi_d, D + hd * 128), (2, hwi_d, 2 * D + hd * 128),
                                  (3, hwi_d, 3 * D + hd * 128)])
            sA_out = [[(0, hwo_d, m * 128)] for m in range(KC)] * NT
            if ctx.plan:
                planA.extend(sA_in)
                planA.extend(sA_out)
                return
            prenorm(OFF_G + (6 + 2) * 8)
            for t in range(NT):
                def make_head(hd):
                    base = (hd % 2) * 4

                    def reg(i):
                        cc = base + i // 2
                        tt = i % 2
                        return y.t[:, cc, tt * TT:(tt + 1) * TT], y.R(cc, tt)
                    A, rA = reg(0)
                    Bf, rB = reg(1)
                    C, rC = reg(2)
                    Dc, rD = reg(3)
                    E, rE = reg(4)
                    Fk, rF = reg(5)
                    Gg, rG = reg(6)
                    Hh, rH = reg(7)
                    D3 = Dc.rearrange("p (c t) -> p c t", t=128)
                    E3 = E.rearrange("p (c t) -> p c t", t=128)
                    F3 = Fk.rearrange("p (c t) -> p c t", t=128)
                    H = {}

                    def projPE():
                        slot = strA.next()
                        pss = []
                        for sub in range(4):
                            ps, pr = bank()
                            kb.grp("pe", [lambda e, k=k: e.matmul(ps[:], lhsT=slot.t[:, sub, k, :], rhs=xn.t[:, k, tsl(t)],
                                                                  start=(k == 0), stop=(k == KC - 1)) for k in range(KC)],
                                   r=[slot.res] + [xn.R(k, t) for k in range(KC)], w=[pr])
                            pss.append((ps, pr))
                        H["pss"] = pss

                    def evac():
                        (pq, prq), (pf, prf), (pi, pri), (pgt, prgt) = H["pss"]
                        kb.op("act", lambda e: e.activation(out=A, in_=pq[:], func=AF.Silu), r=[prq], w=[rA])
                        kb.op("act", lambda e: e.activation(out=sg.t[:, hd, :], in_=pgt[:], func=AF.Silu), r=[prgt], w=[sg.R(hd)])
                        kb.op("act", lambda e: e.activation(out=Bf, in_=pf[:], func=AF.Tanh, scale=0.5), r=[prf], w=[rB])

                        Hb = Hh.bitcast(BF16)[:, 0:TT]
                        kb.op("dve", lambda e: e.tensor_copy(out=Hb, in_=pi[:]), r=[pri], w=[rH])
                        ps2, pr2 = bank()
                        p2b = ps2[:].bitcast(BF16)[:, 0:TT]
                        kb.grp("pe", [lambda e, ch=ch: e.transpose(p2b[:, ch * 128:(ch + 1) * 128], Hb[:, ch * 128:(ch + 1) * 128], identb[:])
                                      for ch in range(4)], r=[rH], w=[pr2])
                        kb.op("act", lambda e: e.activation(out=vt.t[:, :, hd, :], in_=p2b.rearrange("p (c t) -> p c t", t=128),
                                                            func=AF.Copy), r=[pr2], w=[vt.R(hd)])

                    def st_ln():
                        kb.op("act", lambda e: e.activation(out=Gg, in_=Bf, func=AF.Ln, scale=hl[:, hd:hd + 1], bias=b1[:, hd:hd + 1]),
                              r=[rB], w=[rG])

                    def st_c():
                        kb.op("dve", lambda e: e.tensor_scalar(out=C, in0=Bf, scalar1=c1[:, hd:hd + 1], scalar2=hl[:, hd:hd + 1],
                                                               op0=ALU.mult, op1=ALU.add), r=[rB], w=[rC])

                    def st_scan():
                        kb.op("dve", lambda e: e.tensor_tensor_scan(out=Dc, data0=onesf[:], data1=Gg, initial=0.0, op0=ALU.mult,
                                                                    op1=ALU.add), r=[rG], w=[rD])

                    def st_e():
                        kb.op("dve", lambda e: e.tensor_tensor(out=E3, in0=D3, in1=D3[:, :, 63:64].to_broadcast([128, 4, 128]),
                                                               op=ALU.subtract), r=[rD], w=[rE])

                    def st_f():
                        kb.op("dve", lambda e: e.tensor_tensor(out=F3, in0=D3[:, :, 127:128].to_broadcast([128, 4, 128]), in1=D3,
                                                               op=ALU.subtract), r=[rD], w=[rF])

                    def st_tiny():
                        s1 = sm.next()
                        H["s1"] = s1
                        kb.op("dve", lambda e: e.tensor_copy(out=s1.t[:, 0:1], in_=D3[:, 0, 127:128]), r=[rD], w=[s1.R(0)])
                        kb.op("dve", lambda e: e.tensor_tensor(out=s1.t[:, 1:4], in0=D3[:, 1:4, 127], in1=D3[:, 0:3, 127], op=ALU.subtract),
                              r=[rD, s1.R(0)], w=[s1.R(0)])
                        kb.op("dve", lambda e: e.tensor_copy(out=s1.t[:, 4:5], in_=D3[:, 0, 63:64]), r=[rD, s1.R(0)], w=[s1.R(0)])
                        kb.op("dve", lambda e: e.tensor_tensor(out=s1.t[:, 5:8], in0=D3[:, 1:4, 63], in1=D3[:, 0:3, 127], op=ALU.subtract),
                              r=[rD, s1.R(0)], w=[s1.R(0)])

                    def st_exps():
                        s1 = H["s1"]
                        kb.op("act", lambda e: e.activation(out=dec.t[:, hd, :], in_=s1.t[:, 0:4], func=AF.Exp), r=[s1.R(0)], w=[dec.R(hd)])
                        kb.op("act", lambda e: e.activation(out=ssc.t[:, hd, :], in_=s1.t[:, 4:8], func=AF.Exp), r=[s1.R(0)], w=[ssc.R(hd)])

                    def st_g():
                        kb.op("act", lambda e: e.activation(out=Gg, in_=E, func=AF.Exp, scale=-1.0), r=[rE], w=[rG])

                    def st_ee():
                        kb.op("act", lambda e: e.activation(out=E, in_=E, func=AF.Exp), r=[rE, rG], w=[rE])

                    def st_ef():
                        kb.op("act", lambda e: e.activation(out=Fk, in_=Fk, func=AF.Exp), r=[rF], w=[rF])

                    def st_q():
                        kb.op("dve", lambda e: e.tensor_tensor(out=qt.t[:, hd, :], in0=A, in1=E, op=ALU.mult), r=[rA, rE], w=[qt.R(hd)])

                    def st_k():
                        kb.op("dve", lambda e: e.tensor_tensor(out=kt.t[:, hd, :], in0=C, in1=Gg, op=ALU.mult), r=[rC, rG], w=[kt.R(hd)])

                    Eb = E.bitcast(BF16)[:, 0:TT]

                    def st_kd():
                        kb.op("dve", lambda e: e.tensor_tensor(out=Eb, in0=C, in1=Fk, op=ALU.mult), r=[rC, rF], w=[rE])

                    def tail():
                        ps, pr = bank()
                        pb = ps[:].bitcast(BF16)[:, 0:TT]
                        kb.grp("pe", [lambda e, ch=ch: e.transpose(pb[:, ch * 128:(ch + 1) * 128], Eb[:, ch * 128:(ch + 1) * 128], identb[:])
                                      for ch in range(4)], r=[rE], w=[pr])
                        kb.op("dve", lambda e: e.tensor_copy(out=kd.t[:, :, hd, :], in_=pb.rearrange("p (c t) -> p c t", t=128)),
                              r=[pr], w=[kd.R(hd)])

                    steps_a = [st_ln, st_scan, st_c, st_e, st_f, st_tiny, st_exps, st_g, st_ee, st_ef, st_q]
                    steps_b = [st_k, st_kd]
                    return projPE, evac, steps_a, steps_b, tail

                heads = [make_head(hd) for hd in range(KC)]
                heads[0][0]()
                heads[0][1]()
                heads[1][0]()
                heads[1][1]()
                for p in range(KC // 2):
                    h0, h1 = heads[2 * p], heads[2 * p + 1]
                    nxt = p < KC // 2 - 1
                    for i, (sa, sb_) in enumerate(zip(h0[2], h1[2])):
                        sa()
                        sb_()
                        if i == 1 and nxt:
                            heads[2 * p + 2][0]()
                            heads[2 * p + 3][0]()
                    if nxt:
                        heads[2 * p + 2][1]()
                        heads[2 * p + 3][1]()
                    for sa, sb_ in zip(h0[3], h1[3]):
                        sa()
                        sb_()
                    h0[4]()
                    h1[4]()

                allh = list(range(KC))
                CH = {}

                def ph1(ch):
                    cs = slice(ch * 128, (ch + 1) * 128)
                    pA, prA = bank()
                    pB, prB = bank()
                    for half, (pp, ppr) in enumerate(((pA, prA), (pB, prB))):
                        kb.grp("pe", [lambda e, i=i: e.matmul(pp[:, i * 128:(i + 1) * 128], lhsT=kt.t[:, half * 4 + i, cs],
                                                              rhs=qt.t[:, half * 4 + i, cs], start=True, stop=True) for i in range(4)],
                               r=[kt.R(half * 4 + i) for i in range(4)] + [qt.R(half * 4 + i) for i in range(4)], w=[ppr])
                        kb.op("dve", lambda e: e.tensor_tensor(out=scT.t[:, half * 4:half * 4 + 4, :],
                                                               in0=pp[:].rearrange("p (c t) -> p c t", t=128),
                                                               in1=cmask[:].unsqueeze(1).to_broadcast([128, 4, 128]), op=ALU.mult),
                              r=[ppr], w=[scT.R(half)])

                def ph2(ch):
                    cs = slice(ch * 128, (ch + 1) * 128)
                    for hh in allh:
                        kb.op("act", lambda e, hh=hh: e.activation(out=Sbf.t[:, hh, :], in_=S.t[:, hh, :], func=AF.Copy,
                                                                   scale=ssc.t[:, hh, ch:ch + 1]),
                              r=[S.R(hh), ssc.R(hh)], w=[Sbf.R(hh)])
                    pO = [bank(), bank()]
                    for half, (pp, ppr) in enumerate(pO):
                        fns = []
                        for i in range(4):
                            hd = half * 4 + i
                            fns.append(lambda e, i=i, hd=hd: e.matmul(pp[:, i * 128:(i + 1) * 128], lhsT=vt.t[:, ch, hd, :],
                                                                      rhs=scT.t[:, hd, :], start=True, stop=False))
                            fns.append(lambda e, i=i, hd=hd: e.matmul(pp[:, i * 128:(i + 1) * 128], lhsT=Sbf.t[:, hd, :],
                                                                      rhs=qt.t[:, hd, cs], start=False, stop=True))
                        kb.grp("pe", fns, r=[vt.R(half * 4 + i) for i in range(4)] + [scT.R(half)] + [Sbf.R(half * 4 + i) for i in range(4)] +
                               [qt.R(half * 4 + i) for i in range(4)], w=[ppr])
                    pK = [bank(), bank()]
                    for half, (pp, ppr) in enumerate(pK):
                        kb.grp("pe", [lambda e, i=i: e.matmul(pp[:, i * 128:(i + 1) * 128], lhsT=kd.t[:, ch, half * 4 + i, :],
                                                              rhs=vt.t[:, ch, half * 4 + i, :], start=True, stop=True) for i in range(4)],
                               r=[kd.R(half * 4 + i) for i in range(4)] + [vt.R(half * 4 + i) for i in range(4)], w=[ppr])
                    for half, (pp, ppr) in enumerate(pK):
                        for i in range(4):
                            hh = half * 4 + i
                            kb.op("dve", lambda e, hh=hh, i=i: e.scalar_tensor_tensor(out=S.t[:, hh, :], in0=S.t[:, hh, :],
                                                                                     scalar=dec.t[:, hh, ch:ch + 1],
                                                                                     in1=pp[:, i * 128:(i + 1) * 128],
                                                                                     op0=ALU.mult, op1=ALU.add),
                                  r=[ppr, S.R(hh), dec.R(hh)], w=[S.R(hh)])
                    CH[ch] = pO

                def ph3(ch):
                    cs = slice(ch * 128, (ch + 1) * 128)
                    pO = CH[ch]
                    for half, (pp, ppr) in enumerate(pO):
                        kb.op("act", lambda e: e.activation(out=sqb[0].t[:, half * 4:half * 4 + 4, 0:128],
                                                            in_=pp[:].rearrange("p (c t) -> p c t", t=128), func=AF.Square),
                              r=[ppr], w=[sqb[0].R(half * 4 + i) for i in range(4)])
                    pS = [bank(), bank()]
                    for half, (pp, ppr) in enumerate(pS):
                        kb.grp("pe", [lambda e, i=i: e.matmul(pp[:, i * 128:(i + 1) * 128], lhsT=ones_h[:], rhs=sqb[0].t[:, half * 4 + i, 0:128],
                                                              start=True, stop=True) for i in range(4)], r=[sqb[0].R(half * 4 + i) for i in range(4)], w=[ppr])
                    for half, (pp, ppr) in enumerate(pS):
                        rs = rsf.next()
                        kb.op("act", lambda e: e.activation(out=rs.t[:], in_=pp[:], func=AF.Ln, bias=epsb[:, 0:1]), r=[ppr], w=[rs.R(0)])
                        kb.op("act", lambda e: e.activation(out=rs.t[:], in_=rs.t[:], func=AF.Exp, scale=-0.5), r=[rs.R(0)], w=[rs.R(0)])
                        po, por = pO[half]
                        kb.op("dve", lambda e: e.scalar_tensor_tensor(out=onf.t[:, half * 4:half * 4 + 4, :],
                                                                      in0=po[:].rearrange("p (c t) -> p c t", t=128),
                                                                      scalar=pcol(OFF_GN), in1=rs.t[:].rearrange("p (c t) -> p c t", t=128),
                                                                      op0=ALU.mult, op1=ALU.mult), r=[por, rs.R(0)], w=[onf.R(half)])
                    kb.op("dve", lambda e: e.tensor_tensor(out=xn.t[:, :, t * TT + ch * 128:t * TT + (ch + 1) * 128], in0=onf.t[:],
                                                            in1=sg.t[:, :, cs], op=ALU.mult),
                          r=[onf.R(0), onf.R(1)] + [sg.R(hh) for hh in allh], w=[xn.R(c, t) for c in range(KC)])


                ph1(0)
                for ch in range(4):
                    ph2(ch)
                    if ch + 1 < 4:
                        ph1(ch + 1)
                    ph3(ch)

            def posth(t):
                postnorm_t(pv, OFF_G + (6 + 3) * 8, t)
                if t == NT - 1:
                    for tt in range(NT):
                        pre_stats(tt)
                    PRE["stats"] = True
            out_proj(strA, lambda slot, k: slot.t[:, 0, k, :], KC, lambda k, t: xn.t[:, k, tsl(t)],
                     lambda t: [xn.R(k, t) for k in range(KC)], None, posth, tile_outer=True)

        ALIAS = {"toks": [], "bufs": []}

        def SB(tensor):
            b = Buf(tensor, init=ALIAS["toks"])
            ALIAS["bufs"].append(b)
            return b

        def stage_barrier():
            if ctx.plan:
                return
            best = {}
            for tok in ALIAS["toks"]:
                best[tok[0].name] = tok
            for b in ALIAS["bufs"]:
                for r in b.res.values():
                    for tok in ([r.w] if r.w is not None else []) + list(r.rs):
                        k = tok[0].name
                        if k not in best or best[k][1] < tok[1]:
                            best[k] = tok
            ALIAS["toks"] = list(best.values())
            ALIAS["bufs"] = []

        def run_all():
            for g in range(NG):
                if g == 0:
                    load_group(g)
                for l in range(2):
                    nst = 99 if DBG_STAGES is None else DBG_STAGES - 3 * l
                    if nst <= 0:
                        continue
                    with ExitStack() as st:
                        h = None
                        if not ctx.plan:
                            h = SB(st.enter_context(nc.sbuf_tensor(f"h_{g}_{l}_0", [128, JC, G], BF16)))
                        ffn(l, 0, h)
                        stage_barrier()
                    if nst <= 1:
                        continue
                    with ExitStack() as st:
                        if l == 0:
                            bufs = None
                            if not ctx.plan:
                                hc = SB(st.enter_context(nc.sbuf_tensor(f"hc_{g}", [128, KC, 32 + G], BF16)))
                                diag = [SB(st.enter_context(nc.sbuf_tensor(f"dg{i}_{g}", [128, CW, 128], BF16))) for i in range(2)]
                                hb = None
                                bufs = (hc, diag, hb)
                            conv_mixer(*(bufs if bufs else (None, None, None)))
                        else:
                            bufs = (None,) * 9
                            if not ctx.plan:
                                mk = lambda nm, shp, dty: SB(st.enter_context(nc.sbuf_tensor(f"{nm}_{g}", shp, dty)))
                                bufs = (mk("qt", [128, KC, TT], BF16), mk("kt", [128, KC, TT], BF16), mk("sg", [128, KC, TT], BF16),
                                        mk("kd", [128, 4, KC, 128], BF16), mk("vt", [128, 4, KC, 128], BF16),
                                        mk("scT", [128, KC, 128], BF16), mk("Sbf", [128, KC, 128], BF16),
                                        None, mk("onf", [128, KC, 128], F32))
                            hgrn_mixer(*bufs)
                        stage_barrier()
                    if nst <= 2:
                        continue
                    with ExitStack() as st:
                        h = None
                        if not ctx.plan:
                            h = SB(st.enter_context(nc.sbuf_tensor(f"h_{g}_{l}_1", [128, JC, G], BF16)))
                        ffn(l, 1, h)
                        stage_barrier()
                for tl in range(NT):
                    tbs = range(tl * (TT // 128), (tl + 1) * (TT // 128))
                    store_group(g, tbs)
                    if g + 1 < NG:
                        load_group(g + 1, tbs)
                        if not ctx.plan:
                            pre_stats(tl)
                            PRE["stats"] = True

        ctx.plan = True
        run_all()
        strA = WStream(kb, wA, planA, 2, issueA)
        strO = WStream(kb, wO, planO, 1, issueO)
        ctx.plan = False
        run_all()
        sp = kb.eng["sp"]
        for s in io + [q.slot for q in ystg]:
            if s.cnt:
                sp.obj.wait_ge(s.sem, s.cnt)
        kb.barrier(("pe", "act", "dve", "pool"))
        kb.counts = {k: v.count for k, v in kb.eng.items()}
        build_program.last_counts = kb.counts
    return nc


def host_inputs(x_seq, norm_gains, ffn_w_in, ffn_w_out, conv_w_in, conv_b_in, conv_w_dw, conv_b_dw, conv_ln_g, conv_ln_b,
                conv_w_out, conv_b_out, hgrn_w_in, hgrn_lb_logits, hgrn_g_norm, hgrn_w_out):
    f = lambda a: np.ascontiguousarray(np.asarray(a, dtype=np.float32))
    pvec = np.zeros((NPV, 128), np.float32)
    pvec[OFF_G:OFF_G + 96] = f(norm_gains).reshape(96, 128)
    pvec[OFF_CBIN:OFF_CBIN + 16] = f(conv_b_in).reshape(16, 128)
    pvec[OFF_CBDW:OFF_CBDW + 8] = f(conv_b_dw).reshape(8, 128)
    pvec[OFF_LNG:OFF_LNG + 8] = f(conv_ln_g).reshape(8, 128)
    pvec[OFF_LNB:OFF_LNB + 8] = f(conv_ln_b).reshape(8, 128)
    pvec[OFF_CBO:OFF_CBO + 8] = f(conv_b_out).reshape(8, 128)
    pvec[OFF_WDW:OFF_WDW + 248] = f(conv_w_dw).reshape(248, 128)
    pvec[OFF_LBL:OFF_LBL + 16] = f(hgrn_lb_logits).reshape(16, 128)
    pvec[OFF_GN:OFF_GN + 1] = f(hgrn_g_norm).reshape(1, 128)
    return {
        "x": f(x_seq), "ffn_w_in": f(ffn_w_in), "ffn_w_out": f(ffn_w_out), "conv_w_in": f(conv_w_in)[0],
        "conv_w_out": f(conv_w_out)[0], "hgrn_w_in": f(hgrn_w_in)[0], "hgrn_w_out": f(hgrn_w_out)[0],
        "pvec": pvec, "ident": np.eye(128, dtype=np.float32), "cmask": np.triu(np.ones((128, 128), np.float32)),
    }


_NC_CACHE = {}


def kernel(x, norm_gains, ffn_w_in, ffn_w_out, conv_w_in, conv_b_in, conv_w_dw, conv_b_dw, conv_ln_g, conv_ln_b,
           conv_w_out, conv_b_out, hgrn_w_in, hgrn_lb_logits, hgrn_g_norm, hgrn_w_out, _seq_t=None):
    x = np.asarray(x, dtype=np.float32)
    B, S_, _ = x.shape
    seq_t = S_ if _seq_t is None else _seq_t
    if seq_t not in _NC_CACHE:
        _NC_CACHE[seq_t] = build_program(seq_t)
    nc = _NC_CACHE[seq_t]
    in_maps = []
    for c in range(NCORES):
        in_maps.append(host_inputs(x[c % B, :seq_t], norm_gains, ffn_w_in, ffn_w_out, conv_w_in, conv_b_in, conv_w_dw, conv_b_dw,
                                   conv_ln_g, conv_ln_b, conv_w_out, conv_b_out, hgrn_w_in, hgrn_lb_logits, hgrn_g_norm, hgrn_w_out))
    res = run_bass_kernel_spmd(nc, in_maps, core_ids=list(range(NCORES)))
    out = np.stack([np.asarray(res.results[b]["y"], dtype=np.float32) for b in range(B)], axis=0)
    return out
```

```python
import numpy as np
from contextlib import ExitStack
import concourse.bass as bass
import concourse.mybir as mybir
from concourse.bass_utils import run_bass_kernel_spmd

F32 = mybir.dt.float32
BF16 = mybir.dt.bfloat16
AF = mybir.ActivationFunctionType
ALU = mybir.AluOpType

D = 1024
KC = 8
DFF = 2816
JC = 22
G = 1024
TT = 512
NT = G // TT
SEQ = 8192
CW = 31
EPS = 1e-6
NCORES = 2
NIO = 2
SAME_SYNC = True
DBG_STAGES = None
DBG_SUB = 99
DBGX = ''

OFF_G, OFF_CBIN, OFF_CBDW, OFF_LNG, OFF_LNB, OFF_CBO, OFF_WDW, OFF_LBL, OFF_GN = 0, 96, 112, 120, 128, 136, 144, 392, 408
NPV = 512


class Res:
    __slots__ = ("w", "rs", "x")

    def __init__(self, x=False):
        self.w = None
        self.rs = []
        self.x = x


class Buf:
    def __init__(self, t, init=None):
        self.t = t
        self.res = {}
        self.init = init or []

    def R(self, *key):
        r = self.res.get(key)
        if r is None:
            r = self.res[key] = Res()
            r.rs = list(self.init)
        return r


class Slot:
    def __init__(self, t, sem):
        self.t = t
        self.sem = sem
        self.cnt = 0
        self.res = Res()


class Eng:
    def __init__(self, obj, sem):
        self.obj = obj
        self.sem = sem
        self.count = 0
        self.seen = {}


class KB:
    def __init__(self, nc, es):
        self.nc = nc
        self.es = es
        self.eng = {}
        self.nsem = 0

    def sem(self, name):
        self.nsem += 1
        return self.es.enter_context(self.nc.semaphore(name))

    def add_engine(self, name, obj):
        self.eng[name] = Eng(obj, self.sem("e_" + name))

    def _wait(self, e, r, w):
        strong, weak, sems = {}, {}, {}

        def add(d, tok):
            if tok is None:
                return
            k = tok[0].name
            sems[k] = tok[0]
            if d.get(k, 0) < tok[1]:
                d[k] = tok[1]

        for x in r:
            add(strong, x.w)
            if x.x:
                for t in x.rs:
                    add(weak, t)
        for x in w:
            add(strong, x.w)
            for t in x.rs:
                add(weak, t)
        for k, sem in sems.items():
            need = strong.get(k, 0)
            if SAME_SYNC or sem is not e.sem:
                need = max(need, weak.get(k, 0))
            if need > 0 and e.seen.get(k, 0) < need:
                e.obj.wait_ge(sem, need)
                e.seen[k] = need

    @staticmethod
    def _mark(tok, r, w):
        for x in r:
            x.rs.append(tok)
        for x in w:
            x.w = tok
            x.rs = []

    def op(self, en, fn, r=(), w=()):
        e = self.eng[en]
        self._wait(e, r, w)
        ins = fn(e.obj)
        e.count += 1
        ins.then_inc(e.sem, 1)
        self._mark((e.sem, e.count), r, w)

    def grp(self, en, fns, r=(), w=()):
        e = self.eng[en]
        self._wait(e, r, w)
        ins = None
        for fn in fns:
            ins = fn(e.obj)
        e.count += 1
        ins.then_inc(e.sem, 1)
        self._mark((e.sem, e.count), r, w)

    def dma(self, q, fns, slot, r=(), w=()):
        e = self.eng[q]
        self._wait(e, r, w)
        for fn in fns:
            fn(e.obj).then_inc(slot.sem, 16)
            slot.cnt += 16
        self._mark((slot.sem, slot.cnt), r, w)

    def barrier(self, names=("pe", "act", "dve", "pool")):
        for a in names:
            ea = self.eng[a]
            for b in names:
                if a == b:
                    continue
                eb = self.eng[b]
                if eb.count > 0 and ea.seen.get(eb.sem.name, 0) < eb.count:
                    ea.obj.wait_ge(eb.sem, eb.count)
                    ea.seen[eb.sem.name] = eb.count


class WStream:
    def __init__(self, kb, slots, plan, lookahead, issue_fn):
        self.kb, self.slots, self.plan, self.la, self.issue_fn = kb, slots, plan, lookahead, issue_fn
        self.issued = 0
        self.cur = 0

    def _issue_upto(self, lim):
        lim = min(len(self.plan), lim)
        while self.issued < lim:
            self.issue_fn(self.slots[self.issued % len(self.slots)], self.plan[self.issued])
            self.issued += 1

    def next(self, la=None):
        self._issue_upto(self.cur + (self.la if la is None else la) + 1)
        s = self.slots[self.cur % len(self.slots)]
        self.cur += 1
        return s

    def prefetch(self, idx):
        self._issue_upto(idx + 1)


class Rot:
    def __init__(self, items):
        self.items = items
        self.i = 0

    def next(self):
        x = self.items[self.i % len(self.items)]
        self.i += 1
        return x


def build_program(seq_t=SEQ):
    NG = seq_t // G
    nc = bass.Bass("TRN2", target_bir_lowering=False)
    dt = lambda name, shape, kind: nc.dram_tensor(name, shape, F32, kind=kind).ap()
    x_d = dt("x", [seq_t, D], "ExternalInput")
    fwi_d = dt("ffn_w_in", [2, 2, D, 2 * DFF], "ExternalInput")
    fwo_d = dt("ffn_w_out", [2, 2, DFF, D], "ExternalInput")
    cwi_d = dt("conv_w_in", [D, 2 * D], "ExternalInput")
    cwo_d = dt("conv_w_out", [D, D], "ExternalInput")
    hwi_d = dt("hgrn_w_in", [D, 4 * D], "ExternalInput")
    hwo_d = dt("hgrn_w_out", [D, D], "ExternalInput")
    pv_d = dt("pvec", [NPV, 128], "ExternalInput")
    id_d = dt("ident", [128, 128], "ExternalInput")
    cm_d = dt("cmask", [128, 128], "ExternalInput")
    y_d = dt("y", [seq_t, D], "ExternalOutput")

    with ExitStack() as es:
        kb = KB(nc, es)
        kb.add_engine("pe", nc.tensor)
        kb.add_engine("act", nc.scalar)
        kb.add_engine("dve", nc.vector)
        kb.add_engine("pool", nc.gpsimd)
        kb.add_engine("sp", nc.sync)

        def sb(name, shape, dtype=F32):
            return es.enter_context(nc.sbuf_tensor("s_" + name, shape, dtype))

        x = Buf(sb("x", [128, KC, G]))
        xn = Buf(sb("xn", [128, KC, G], BF16))
        y = Buf(sb("y", [128, KC, G]))
        sqb = [Buf(sb(f"sq{i}", [128, KC, TT], BF16)) for i in range(2)]
        wA = [Slot(sb(f"wA{i}", [128, 4, KC, 128], BF16), kb.sem(f"wA{i}")) for i in range(3)]
        wO = [Slot(sb(f"wO{i}", [128, JC, 128], BF16), kb.sem(f"wO{i}")) for i in range(2)]
        S = Buf(sb("S", [128, KC, 128]))
        halo = Buf(sb("halo", [128, KC, 32], BF16))
        pv = sb("pv", [128, NPV])
        pvh = sb("pvh", [128, 96])
        lbv = sb("lbv", [128, 8])
        oml = sb("oml", [128, 8])
        hl = sb("hl", [128, 8])
        b1 = sb("b1", [128, 8])
        c1 = sb("c1", [128, 8])
        ident = sb("ident", [128, 128])
        identb = sb("identb", [128, 128], BF16)
        cmask = sb("cmask", [128, 128])
        ones_d = sb("ones_d", [128, 128], BF16)
        ones_h = sb("ones_h", [128, 128], BF16)
        onesf = sb("onesf", [128, TT])
        mhalf1 = sb("mhalf1", [128, 1])
        epsb = sb("epsb", [128, 1])
        dmy = Buf(sb("dmy", [128, 1]))
        stf = Rot([Buf(sb(f"stf{i}", [128, TT])) for i in range(2)])
        rsf = Rot([Buf(sb(f"rsf{i}", [128, TT])) for i in range(2)])
        io = [Slot(sb(f"io{i}", [128, D]), kb.sem(f"io{i}")) for i in range(NIO)]
        ioR = Rot(io)
        dec = Buf(sb("dec", [128, KC, 4]))
        ssc = Buf(sb("ssc", [128, KC, 4]))
        sm = Rot([Buf(sb(f"sm{i}", [128, 8])) for i in range(4)])
        banks = [(es.enter_context(nc.psum_tensor(f"psum{i}", [128, TT], F32)), Res(True)) for i in range(8)]
        bank_rot = Rot(banks)

        def bank():
            b = bank_rot.next()
            assert b[1].w is None or len(b[1].rs) > 0, "PSUM bank re-allocated before its consumer was emitted"
            return b

        c_res = Res()
        pslot = Slot(None, kb.sem("pload"))
        kb.dma("sp", [lambda e: e.dma_start(out=ident[:], in_=id_d[:, :]),
                      lambda e: e.dma_start(out=cmask[:], in_=cm_d[:, :])], pslot, w=[c_res])
        kb.dma("sp", [lambda e, i=i: e.dma_start(out=io[0].t[:, i * 128:(i + 1) * 128], in_=pv_d[i * 128:(i + 1) * 128, :]) for i in range(4)],
               io[0], w=[io[0].res])
        ps, pr = bank()
        kb.grp("pe", [lambda e, i=i: e.transpose(ps[:, i * 128:(i + 1) * 128], io[0].t[:, i * 128:(i + 1) * 128], ident[:]) for i in range(4)],
               r=[c_res, io[0].res], w=[pr])
        pv_res = Res()
        kb.op("dve", lambda e: e.tensor_copy(out=pv[:], in_=ps[:]), r=[pr], w=[pv_res])
        kb.op("dve", lambda e: e.tensor_scalar(out=pvh[:], in0=pv[:, 0:96], scalar1=0.5, scalar2=None, op0=ALU.mult),
              r=[pv_res], w=[pv_res])
        kb.op("dve", lambda e: e.tensor_tensor(out=lbv[:], in0=pv[:, OFF_LBL + 8:OFF_LBL + 16], in1=pv[:, OFF_LBL:OFF_LBL + 8],
                                               op=ALU.subtract), r=[pv_res], w=[pv_res])
        kb.op("act", lambda e: e.activation(out=lbv[:], in_=lbv[:], func=AF.Sigmoid), r=[pv_res], w=[pv_res])
        kb.op("dve", lambda e: e.tensor_scalar(out=oml[:], in0=lbv[:], scalar1=-1.0, scalar2=1.0, op0=ALU.mult, op1=ALU.add),
              r=[pv_res], w=[pv_res])
        kb.op("dve", lambda e: e.tensor_scalar(out=hl[:], in0=oml[:], scalar1=0.5, scalar2=None, op0=ALU.mult), r=[pv_res], w=[pv_res])
        kb.op("dve", lambda e: e.tensor_tensor(out=b1[:], in0=hl[:], in1=lbv[:], op=ALU.add), r=[pv_res], w=[pv_res])
        kb.op("dve", lambda e: e.tensor_scalar(out=c1[:], in0=hl[:], scalar1=-1.0, scalar2=None, op0=ALU.mult), r=[pv_res], w=[pv_res])
        kb.op("dve", lambda e: e.tensor_copy(out=identb[:], in_=ident[:]), r=[c_res], w=[c_res])
        kb.op("dve", lambda e: e.memset(ones_d[:], 1.0 / 1024.0), w=[c_res])
        kb.op("dve", lambda e: e.memset(ones_h[:], 1.0 / 128.0), w=[c_res])
        kb.op("dve", lambda e: e.memset(onesf[:], 1.0), w=[c_res])
        kb.op("dve", lambda e: e.memset(mhalf1[:], -0.5), w=[c_res])
        kb.op("dve", lambda e: e.memset(epsb[:], EPS), w=[c_res])
        mhalf = mhalf1[:].to_broadcast([128, TT])
        kb.op("dve", lambda e: e.memset(S.t[:], 0.0), w=[S.R(hh) for hh in range(KC)])
        kb.op("dve", lambda e: e.memset(halo.t[:], 0.0), w=[halo.R(0)])
        kb.eng["sp"].obj.wait_ge(pslot.sem, pslot.cnt)
        kb.barrier(("pe", "act", "dve", "pool"))
        for en in ("pe", "act", "pool"):
            e = kb.eng[en]
            e.obj.wait_ge(kb.eng["dve"].sem, kb.eng["dve"].count)
            e.seen[kb.eng["dve"].sem.name] = kb.eng["dve"].count

        def pcol(i):
            return pv[:, i:i + 1]

        planA, planO = [], []

        def issueA(slot, spec):
            fns = []
            for (sub, w2d, col0) in spec:
                src = w2d.rearrange("(kc p) n -> p kc n", p=128)[:, :, col0:col0 + 128]
                fns.append(lambda e, sub=sub, src=src: e.dma_start(out=slot.t[:, sub, :, :], in_=src))
            kb.dma("pool", fns, slot, w=[slot.res])

        def issueO(slot, spec):
            w2d, col0 = spec
            src = w2d.rearrange("(j p) n -> p j n", p=128)[:, :, col0:col0 + 128]
            kb.dma("pool", [lambda e: e.dma_start(out=slot.t[:], in_=src)], slot, w=[slot.res])

        class Ctx:
            plan = True
        ctx = Ctx()

        def tsl(t):
            return slice(t * TT, (t + 1) * TT)

        def rstd_from_sq(sq, onesmat, eps=EPS, dst=None):
            ps, pr = bank()
            kb.grp("pe", [lambda e, c=c: e.matmul(ps[:], lhsT=onesmat[:], rhs=sq.t[:, c, :], start=(c == 0), stop=(c == KC - 1))
                          for c in range(KC)], r=[sq.R(c) for c in range(KC)], w=[pr])
            t = rsf.next() if dst is None else dst
            kb.op("act", lambda e: e.activation(out=t.t[:], in_=ps[:], func=AF.Ln, bias=epsb[:, 0:1]), r=[pr], w=[t.R(0)])
            kb.op("act", lambda e: e.activation(out=t.t[:], in_=t.t[:], func=AF.Exp, scale=-0.5), r=[t.R(0)], w=[t.R(0)])
            return t

        PRE = {"stats": False}

        def pre_stats(t):
            sq = sqb[t % 2]
            kb.op("act", lambda e: e.activation(out=sq.t[:], in_=x.t[:, :, tsl(t)], func=AF.Square),
                  r=[x.R(c, t) for c in range(KC)], w=[sq.R(c) for c in range(KC)])
            rstd_from_sq(sq, ones_d, dst=stf.items[t])

        def pre_apply(gcol, t):
            rs = stf.items[t]
            for c in range(KC):
                kb.op("dve", lambda e, c=c: e.scalar_tensor_tensor(out=xn.t[:, c, tsl(t)], in0=x.t[:, c, tsl(t)],
                                                                    scalar=pcol(gcol + c), in1=rs.t[:], op0=ALU.mult, op1=ALU.mult),
                      r=[x.R(c, t), rs.R(0)], w=[xn.R(c, t)])

        def prenorm(gcol):
            for t in range(NT):
                if not PRE["stats"]:
                    pre_stats(t)
                pre_apply(gcol, t)
            PRE["stats"] = False

        def postnorm_t(gsrc, gcol, t):
            rs = rstd_from_sq(sqb[t % 2], ones_d)
            for c in range(KC):
                kb.op("dve", lambda e, c=c: e.scalar_tensor_tensor(out=y.t[:, c, tsl(t)], in0=y.t[:, c, tsl(t)],
                                                                    scalar=gsrc[:, gcol + c:gcol + c + 1], in1=rs.t[:],
                                                                    op0=ALU.mult, op1=ALU.mult),
                      r=[y.R(c, t), rs.R(0)], w=[y.R(c, t)])
                kb.op("dve", lambda e, c=c: e.tensor_tensor(out=x.t[:, c, tsl(t)], in0=x.t[:, c, tsl(t)], in1=y.t[:, c, tsl(t)],
                                                            op=ALU.add), r=[y.R(c, t), x.R(c, t)], w=[x.R(c, t)])

        def out_proj(stream, lhs_fn, nk, rhs_fn, rhs_res_fn, bias_col, post, tile_outer=False):
            kb.op("act", lambda e: e.activation(out=dmy.t[:], in_=epsb[:, 0:1], func=AF.Exp), w=[dmy.R(0)])

            def one(slot, m, t):
                ps, pr = bank()
                kb.grp("pe", [lambda e, k=k: e.matmul(ps[:], lhsT=lhs_fn(slot, k), rhs=rhs_fn(k, t), start=(k == 0), stop=(k == nk - 1))
                              for k in range(nk)], r=[slot.res] + rhs_res_fn(t), w=[pr])
                if bias_col is None:
                    kb.op("act", lambda e: e.activation(out=y.t[:, m, tsl(t)], in_=ps[:], func=AF.Copy), r=[pr], w=[y.R(m, t)])
                    kb.op("act", lambda e: e.activation(out=sqb[t % 2].t[:, m, :], in_=ps[:], func=AF.Square),
                          r=[pr], w=[sqb[t % 2].R(m)])
                else:
                    kb.op("act", lambda e: e.activation(out=y.t[:, m, tsl(t)], in_=ps[:], func=AF.Identity,
                                                        bias=pcol(bias_col + m)), r=[pr], w=[y.R(m, t)])
                    kb.op("act", lambda e: e.activation(out=sqb[t % 2].t[:, m, :], in_=ps[:], func=AF.Square,
                                                        bias=pcol(bias_col + m)), r=[pr], w=[sqb[t % 2].R(m)])

            if tile_outer:
                for t in range(NT):
                    for m in range(KC):
                        one(stream.next(), m, t)
                    post(t)
                return
            for m in range(KC - 2):
                slot = stream.next()
                for t in range(NT):
                    one(slot, m, t)
            base = stream.cur
            sa = stream.next()
            sb_ = stream.next(la=stream.la - 1)
            one(sa, KC - 2, 0)
            one(sb_, KC - 1, 0)
            one(sa, KC - 2, 1)
            stream.prefetch(base + 1 + stream.la)
            post(0)
            one(sb_, KC - 1, 1)
            post(1)

        class Stg:
            def __init__(self, ap, rl, slot):
                self.ap, self.rl, self.slot = ap, rl, slot

        ystg = [Stg(y.t[:, c, :], [y.R(c, 0), y.R(c, 1)], Slot(None, kb.sem(f"ys{c}"))) for c in range(KC)]
        stg_store = Rot(ystg[0:4] + [Stg(io[i].t[:], [io[i].res], io[i]) for i in range(NIO)])
        stg_load = Rot(ystg[4:8])

        def load_group(g, tbs=range(G // 128)):
            if ctx.plan:
                return
            for tb in tbs:
                st = stg_load.next()
                r0 = g * G + tb * 128
                kb.dma("sp", [lambda e: e.dma_start(out=st.ap, in_=x_d[r0:r0 + 128, :])], st.slot, w=st.rl)
                for half in range(2):
                    ps, pr = bank()
                    kb.grp("pe", [lambda e, i=i: e.transpose(ps[:, i * 128:(i + 1) * 128],
                                                             st.ap[:, (half * 4 + i) * 128:(half * 4 + i + 1) * 128], ident[:])
                                  for i in range(4)], r=st.rl, w=[pr])
                    dst = x.t[:, half * 4:half * 4 + 4, tb * 128:(tb + 1) * 128]
                    src = ps[:].rearrange("p (c t) -> p c t", t=128)
                    wr = [x.R(half * 4 + i, tb // 4) for i in range(4)]
                    if half == 0:
                        kb.op("act", lambda e: e.activation(out=dst, in_=src, func=AF.Copy), r=[pr], w=wr)
                    else:
                        kb.op("dve", lambda e: e.tensor_copy(out=dst, in_=src), r=[pr], w=wr)

        def store_group(g, tbs=range(G // 128)):
            if ctx.plan:
                return
            for tb in tbs:
                st = stg_store.next()
                for half in range(2):
                    ps, pr = bank()
                    kb.grp("pe", [lambda e, i=i: e.transpose(ps[:, i * 128:(i + 1) * 128],
                                                             x.t[:, half * 4 + i, tb * 128:(tb + 1) * 128], ident[:])
                                  for i in range(4)], r=[x.R(half * 4 + i, tb // 4) for i in range(4)], w=[pr])
                    dst = st.ap[:, half * 512:(half + 1) * 512]
                    if half == 0:
                        kb.op("act", lambda e: e.activation(out=dst, in_=ps[:], func=AF.Copy), r=[pr], w=st.rl)
                    else:
                        kb.op("dve", lambda e: e.tensor_copy(out=dst, in_=ps[:]), r=[pr], w=st.rl)
                r0 = g * G + tb * 128
                kb.dma("sp", [lambda e: e.dma_start(out=y_d[r0:r0 + 128, :], in_=st.ap)], st.slot, r=st.rl)

        def ffn(l, i, h):
            win = fwi_d[l, i]
            wout = fwo_d[l, i]
            sA = [[(0, win, j * 128), (1, win, DFF + j * 128)] for j in range(JC)]
            sO = [(wout, m * 128) for m in range(KC)]
            if ctx.plan:
                planA.extend(sA)
                planO.extend(sO)
                return
            prenorm(OFF_G + (l * 6 + (0 if i == 0 else 4)) * 8)
            def hidden(slot, j, t):
                pg, prg = bank()
                kb.grp("pe", [lambda e, k=k: e.matmul(pg[:], lhsT=slot.t[:, 0, k, :], rhs=xn.t[:, k, tsl(t)],
                                                      start=(k == 0), stop=(k == KC - 1)) for k in range(KC)],
                       r=[slot.res] + [xn.R(k, t) for k in range(KC)], w=[prg])
                pu, pru = bank()
                kb.grp("pe", [lambda e, k=k: e.matmul(pu[:], lhsT=slot.t[:, 1, k, :], rhs=xn.t[:, k, tsl(t)],
                                                      start=(k == 0), stop=(k == KC - 1)) for k in range(KC)],
                       r=[slot.res] + [xn.R(k, t) for k in range(KC)], w=[pru])
                st = stf.next()
                kb.op("act", lambda e: e.activation(out=st.t[:], in_=pg[:], func=AF.Silu), r=[prg], w=[st.R(0)])
                kb.op("dve", lambda e: e.tensor_tensor(out=h.t[:, j, tsl(t)], in0=st.t[:], in1=pu[:], op=ALU.mult),
                      r=[st.R(0), pru], w=[h.R(j, t)])

            HEADSTART = False
            if HEADSTART:
                base = strA.cur
                s3 = [strA.next(la=2 - i) for i in range(3)]
                for i in range(3):
                    hidden(s3[i], i, 0)
                for i in range(3):
                    hidden(s3[i], i, 1)
                    strA.prefetch(base + 3 + i)
            for j in range(3 if HEADSTART else 0, JC):
                slot = strA.next()
                for t in range(NT):
                    hidden(slot, j, t)

            last = (l == 1 and i == 1)

            def post(t):
                postnorm_t(pvh, (l * 6 + (1 if i == 0 else 5)) * 8, t)
                if not last and t == NT - 1:
                    for tt in range(NT):
                        pre_stats(tt)
                    PRE["stats"] = True
            out_proj(strO, lambda slot, k: slot.t[:, k, :], JC, lambda k, t: h.t[:, k, tsl(t)],
                     lambda t: [h.R(k, t) for k in range(JC)], None,
                     post)

        def conv_mixer(hc, diag, hb):
            sA_in = [[(0, cwi_d, c * 128), (1, cwi_d, D + c * 128)] for c in range(KC)]
            sA_out = [[(0, cwo_d, m * 128)] for m in range(KC)] * NT
            if ctx.plan:
                planA.extend(sA_in)
                planA.extend(sA_out)
                return
            prenorm(OFF_G + 2 * 8)
            kb.op("dve", lambda e: e.tensor_copy(out=hc.t[:, :, 0:32], in_=halo.t[:]), r=[halo.R(0)],
                  w=[hc.R(c, -1) for c in range(KC)])
            if DBG_SUB <= -2:
                return
            for c in range(KC):
                slot = strA.next()
                for t in range(NT):
                    pa, pra = bank()
                    kb.grp("pe", [lambda e, k=k: e.matmul(pa[:], lhsT=slot.t[:, 0, k, :], rhs=xn.t[:, k, tsl(t)],
                                                          start=(k == 0), stop=(k == KC - 1)) for k in range(KC)],
                           r=[slot.res] + [xn.R(k, t) for k in range(KC)], w=[pra])
                    pg, prg = bank()
                    kb.grp("pe", [lambda e, k=k: e.matmul(pg[:], lhsT=slot.t[:, 1, k, :], rhs=xn.t[:, k, tsl(t)],
                                                          start=(k == 0), stop=(k == KC - 1)) for k in range(KC)],
                           r=[slot.res] + [xn.R(k, t) for k in range(KC)], w=[prg])
                    st = stf.next()
                    kb.op("act", lambda e: e.activation(out=st.t[:], in_=pg[:], func=AF.Sigmoid, bias=pcol(OFF_CBIN + 8 + c)),
                          r=[prg], w=[st.R(0)])
                    kb.op("dve", lambda e: e.scalar_tensor_tensor(out=hc.t[:, c, 32 + t * TT:32 + (t + 1) * TT], in0=pa[:],
                                                                  scalar=pcol(OFF_CBIN + c), in1=st.t[:], op0=ALU.add, op1=ALU.mult),
                          r=[pra, st.R(0)], w=[hc.R(c, t)])
            if DBG_SUB <= -1:
                return
            kb.op("dve", lambda e: e.tensor_copy(out=halo.t[:], in_=hc.t[:, :, G:G + 32]),
                  r=[hc.R(c, NT - 1) for c in range(KC)], w=[halo.R(0)])
            if DBG_SUB <= 1:
                return
            def build_diag(c):
                dg = diag[c % 2]
                wsl = pv[:, OFF_WDW + c:OFF_WDW + c + 8 * CW:8]
                kb.op("dve", lambda e: e.tensor_tensor(out=dg.t[:], in0=identb[:].unsqueeze(1).to_broadcast([128, CW, 128]),
                                                       in1=wsl.unsqueeze(2).to_broadcast([128, CW, 128]), op=ALU.mult),
                      w=[dg.R(0)])

            build_diag(0)
            for c in range(KC):
                dg = diag[c % 2]
                if c + 1 < KC:
                    build_diag(c + 1)
                for t in range(NT):
                    ps, pr = bank()
                    rr = [dg.R(0), hc.R(c, t), hc.R(c, t - 1)]
                    kb.grp("pe", [lambda e, k=k: e.matmul(ps[:], lhsT=(identb[:] if 'A' in DBGX else dg.t[:, k, :]),
                                                          rhs=(xn.t[:, c, tsl(t)] if 'D' in DBGX else hc.t[:, c, 32 + t * TT + k - 30:32 + t * TT + k - 30 + TT]),
                                                          start=(k == 0), stop=(k == (7 if 'E' in DBGX else CW - 1))) for k in range(8 if 'E' in DBGX else CW)], r=rr, w=[pr])
                    bcol = pcol(OFF_CBDW + c)
                    if 'B' in DBGX:
                        kb.op("act", lambda e: e.activation(out=y.t[:, c, tsl(t)], in_=ps[:], func=AF.Copy), r=[pr], w=[y.R(c, t)])
                        continue
                    if 'H' not in DBGX and 'I' not in DBGX:
                        kb.op("act", lambda e: e.activation(out=y.t[:, c, tsl(t)], in_=ps[:], func=AF.Identity, bias=bcol),
                              r=[pr], w=[y.R(c, t)])
                    if 'G' not in DBGX and 'I' not in DBGX:
                        kb.op("act", lambda e: e.activation(out=sqb[t % 2].t[:, c, :], in_=ps[:], func=AF.Square, bias=bcol),
                              r=[pr], w=[sqb[t % 2].R(c)])
                    if 'G' not in DBGX and 'H' not in DBGX and 'J' not in DBGX:
                        kb.op("dve", lambda e: e.tensor_scalar(out=xn.t[:, c, tsl(t)], in0=ps[:], scalar1=bcol, scalar2=None, op0=ALU.add),
                              r=[pr], w=[xn.R(c, t)])
            if DBG_SUB <= 2:
                return
            for t in range(NT):
                p1, pr1 = bank()
                kb.grp("pe", [lambda e, c=c: e.matmul(p1[:], lhsT=ones_d[:], rhs=xn.t[:, c, tsl(t)], start=(c == 0), stop=(c == KC - 1))
                              for c in range(KC)], r=[xn.R(c, t) for c in range(KC)], w=[pr1])
                p2, pr2 = bank()
                kb.grp("pe", [lambda e, c=c: e.matmul(p2[:], lhsT=ones_d[:], rhs=sqb[t % 2].t[:, c, :], start=(c == 0), stop=(c == KC - 1))
                              for c in range(KC)], r=[sqb[t % 2].R(c) for c in range(KC)], w=[pr2])
                m1 = stf.next()
                kb.op("dve", lambda e: e.tensor_copy(out=m1.t[:], in_=p1[:]), r=[pr1], w=[m1.R(0)])
                rs = rsf.next()
                kb.op("dve", lambda e: e.tensor_tensor(out=rs.t[:], in0=m1.t[:], in1=m1.t[:], op=ALU.mult), r=[m1.R(0)], w=[rs.R(0)])
                kb.op("dve", lambda e: e.scalar_tensor_tensor(out=rs.t[:], in0=p2[:], scalar=EPS, in1=rs.t[:], op0=ALU.add,
                                                              op1=ALU.subtract), r=[pr2, rs.R(0)], w=[rs.R(0)])
                kb.op("act", lambda e: e.activation(out=rs.t[:], in_=rs.t[:], func=AF.Ln), r=[rs.R(0)], w=[rs.R(0)])
                kb.op("act", lambda e: e.activation(out=rs.t[:], in_=rs.t[:], func=AF.Exp, scale=-0.5), r=[rs.R(0)], w=[rs.R(0)])
                for c in range(KC):
                    kb.op("dve", lambda e, c=c: e.tensor_tensor(out=y.t[:, c, tsl(t)], in0=y.t[:, c, tsl(t)], in1=m1.t[:],
                                                                op=ALU.subtract), r=[y.R(c, t), m1.R(0)], w=[y.R(c, t)])
                    kb.op("dve", lambda e, c=c: e.tensor_tensor(out=y.t[:, c, tsl(t)], in0=y.t[:, c, tsl(t)], in1=rs.t[:],
                                                                 op=ALU.mult), r=[y.R(c, t), rs.R(0)], w=[y.R(c, t)])
                    kb.op("act", lambda e, c=c: e.activation(out=xn.t[:, c, tsl(t)], in_=y.t[:, c, tsl(t)], func=AF.Silu,
                                                             bias=pcol(OFF_LNB + c), scale=pcol(OFF_LNG + c)),
                          r=[y.R(c, t)], w=[xn.R(c, t)])

            def postc(t):
                postnorm_t(pv, OFF_G + 3 * 8, t)
                if t == NT - 1:
                    for tt in range(NT):
                        pre_stats(tt)
                    PRE["stats"] = True
            out_proj(strA, lambda slot, k: slot.t[:, 0, k, :], KC, lambda k, t: xn.t[:, k, tsl(t)],
                     lambda t: [xn.R(k, t) for k in range(KC)], OFF_CBO, postc, tile_outer=True)

        def hgrn_mixer(qt, kt, sg, kd, vt, scT, Sbf, osq, onf):
            sA_in = []
            for t in range(NT):
                for hd in range(KC):
                    sA_in.append([(0, hwi_d, hd * 128), (1, hwi_d, D + hd * 128), (2, hwi_d, 2 * D + hd * 128),
                                  (3, hwi_d, 3 * D + hd * 128)])
            sA_out = [[(0, hwo_d, m * 128)] for m in range(KC)] * NT
            if ctx.plan:
                planA.extend(sA_in)
                planA.extend(sA_out)
                return
            prenorm(OFF_G + (6 + 2) * 8)
            for t in range(NT):
                def make_head(hd):
                    base = (hd % 2) * 4

                    def reg(i):
                        cc = base + i // 2
                        tt = i % 2
                        return y.t[:, cc, tt * TT:(tt + 1) * TT], y.R(cc, tt)
                    A, rA = reg(0)
                    Bf, rB = reg(1)
                    C, rC = reg(2)
                    Dc, rD = reg(3)
                    E, rE = reg(4)
                    Fk, rF = reg(5)
                    Gg, rG = reg(6)
                    Hh, rH = reg(7)
                    D3 = Dc.rearrange("p (c t) -> p c t", t=128)
                    E3 = E.rearrange("p (c t) -> p c t", t=128)
                    F3 = Fk.rearrange("p (c t) -> p c t", t=128)
                    H = {}

                    def projPE():
                        slot = strA.next()
                        pss = []
                        for sub in range(4):
                            ps, pr = bank()
                            kb.grp("pe", [lambda e, k=k: e.matmul(ps[:], lhsT=slot.t[:, sub, k, :], rhs=xn.t[:, k, tsl(t)],
                                                                  start=(k == 0), stop=(k == KC - 1)) for k in range(KC)],
                                   r=[slot.res] + [xn.R(k, t) for k in range(KC)], w=[pr])
                            pss.append((ps, pr))
                        H["pss"] = pss

                    def evac():
                        (pq, prq), (pf, prf), (pi, pri), (pgt, prgt) = H["pss"]
                        kb.op("act", lambda e: e.activation(out=A, in_=pq[:], func=AF.Silu), r=[prq], w=[rA])
                        kb.op("act", lambda e: e.activation(out=sg.t[:, hd, :], in_=pgt[:], func=AF.Silu), r=[prgt], w=[sg.R(hd)])
                        kb.op("act", lambda e: e.activation(out=Bf, in_=pf[:], func=AF.Tanh, scale=0.5), r=[prf], w=[rB])

                        kb.op("dve", lambda e: e.tensor_copy(out=Hh, in_=pi[:]), r=[pri], w=[rH])
                        ps2, pr2 = bank()
                        kb.grp("pe", [lambda e, ch=ch: e.transpose(ps2[:, ch * 128:(ch + 1) * 128], Hh[:, ch * 128:(ch + 1) * 128], ident[:])
                                      for ch in range(4)], r=[rH], w=[pr2])
                        kb.op("act", lambda e: e.activation(out=vt.t[:, :, hd, :], in_=ps2[:].rearrange("p (c t) -> p c t", t=128),
                                                            func=AF.Copy), r=[pr2], w=[vt.R(hd)])

                    def st_ln():
                        kb.op("act", lambda e: e.activation(out=Gg, in_=Bf, func=AF.Ln, scale=hl[:, hd:hd + 1], bias=b1[:, hd:hd + 1]),
                              r=[rB], w=[rG])

                    def st_c():
                        kb.op("dve", lambda e: e.tensor_scalar(out=C, in0=Bf, scalar1=c1[:, hd:hd + 1], scalar2=hl[:, hd:hd + 1],
                                                               op0=ALU.mult, op1=ALU.add), r=[rB], w=[rC])

                    def st_scan():
                        kb.op("dve", lambda e: e.tensor_tensor_scan(out=Dc, data0=onesf[:], data1=Gg, initial=0.0, op0=ALU.mult,
                                                                    op1=ALU.add), r=[rG], w=[rD])

                    def st_e():
                        kb.op("dve", lambda e: e.tensor_tensor(out=E3, in0=D3, in1=D3[:, :, 63:64].to_broadcast([128, 4, 128]),
                                                               op=ALU.subtract), r=[rD], w=[rE])

                    def st_f():
                        kb.op("dve", lambda e: e.tensor_tensor(out=F3, in0=D3[:, :, 127:128].to_broadcast([128, 4, 128]), in1=D3,
                                                               op=ALU.subtract), r=[rD], w=[rF])

                    def st_tiny():
                        s1 = sm.next()
                        H["s1"] = s1
                        kb.op("dve", lambda e: e.tensor_copy(out=s1.t[:, 0:1], in_=D3[:, 0, 127:128]), r=[rD], w=[s1.R(0)])
                        kb.op("dve", lambda e: e.tensor_tensor(out=s1.t[:, 1:4], in0=D3[:, 1:4, 127], in1=D3[:, 0:3, 127], op=ALU.subtract),
                              r=[rD, s1.R(0)], w=[s1.R(0)])
                        kb.op("dve", lambda e: e.tensor_copy(out=s1.t[:, 4:5], in_=D3[:, 0, 63:64]), r=[rD, s1.R(0)], w=[s1.R(0)])
                        kb.op("dve", lambda e: e.tensor_tensor(out=s1.t[:, 5:8], in0=D3[:, 1:4, 63], in1=D3[:, 0:3, 127], op=ALU.subtract),
                              r=[rD, s1.R(0)], w=[s1.R(0)])

                    def st_exps():
                        s1 = H["s1"]
                        kb.op("act", lambda e: e.activation(out=dec.t[:, hd, :], in_=s1.t[:, 0:4], func=AF.Exp), r=[s1.R(0)], w=[dec.R(hd)])
                        kb.op("act", lambda e: e.activation(out=ssc.t[:, hd, :], in_=s1.t[:, 4:8], func=AF.Exp), r=[s1.R(0)], w=[ssc.R(hd)])

                    def st_g():
                        kb.op("act", lambda e: e.activation(out=Gg, in_=E, func=AF.Exp, scale=-1.0), r=[rE], w=[rG])

                    def st_ee():
                        kb.op("act", lambda e: e.activation(out=E, in_=E, func=AF.Exp), r=[rE, rG], w=[rE])

                    def st_ef():
                        kb.op("act", lambda e: e.activation(out=Fk, in_=Fk, func=AF.Exp), r=[rF], w=[rF])

                    def st_q():
                        kb.op("dve", lambda e: e.tensor_tensor(out=qt.t[:, hd, :], in0=A, in1=E, op=ALU.mult), r=[rA, rE], w=[qt.R(hd)])

                    def st_k():
                        kb.op("dve", lambda e: e.tensor_tensor(out=kt.t[:, hd, :], in0=C, in1=Gg, op=ALU.mult), r=[rC, rG], w=[kt.R(hd)])

                    def st_kd():
                        kb.op("dve", lambda e: e.tensor_tensor(out=Fk, in0=C, in1=Fk, op=ALU.mult), r=[rC, rF], w=[rF])

                    def tail():
                        ps, pr = bank()
                        kb.grp("pe", [lambda e, ch=ch: e.transpose(ps[:, ch * 128:(ch + 1) * 128], Fk[:, ch * 128:(ch + 1) * 128], ident[:])
                                      for ch in range(4)], r=[rF], w=[pr])
                        kb.op("dve", lambda e: e.tensor_copy(out=kd.t[:, :, hd, :], in_=ps[:].rearrange("p (c t) -> p c t", t=128)),
                              r=[pr], w=[kd.R(hd)])

                    steps_a = [st_ln, st_scan, st_c, st_e, st_f, st_tiny, st_exps, st_g, st_ee, st_ef, st_q]
                    steps_b = [st_k, st_kd]
                    return projPE, evac, steps_a, steps_b, tail

                heads = [make_head(hd) for hd in range(KC)]
                heads[0][0]()
                heads[0][1]()
                heads[1][0]()
                heads[1][1]()
                for p in range(KC // 2):
                    h0, h1 = heads[2 * p], heads[2 * p + 1]
                    nxt = p < KC // 2 - 1
                    for i, (sa, sb_) in enumerate(zip(h0[2], h1[2])):
                        sa()
                        sb_()
                        if i == 1 and nxt:
                            heads[2 * p + 2][0]()
                            heads[2 * p + 3][0]()
                    if nxt:
                        heads[2 * p + 2][1]()
                        heads[2 * p + 3][1]()
                    for sa, sb_ in zip(h0[3], h1[3]):
                        sa()
                        sb_()
                    h0[4]()
                    h1[4]()

                allh = list(range(KC))
                CH = {}

                def ph1(ch):
                    cs = slice(ch * 128, (ch + 1) * 128)
                    pA, prA = bank()
                    pB, prB = bank()
                    for half, (pp, ppr) in enumerate(((pA, prA), (pB, prB))):
                        kb.grp("pe", [lambda e, i=i: e.matmul(pp[:, i * 128:(i + 1) * 128], lhsT=kt.t[:, half * 4 + i, cs],
                                                              rhs=qt.t[:, half * 4 + i, cs], start=True, stop=True) for i in range(4)],
                               r=[kt.R(half * 4 + i) for i in range(4)] + [qt.R(half * 4 + i) for i in range(4)], w=[ppr])
                        kb.op("dve", lambda e: e.tensor_tensor(out=scT.t[:, half * 4:half * 4 + 4, :],
                                                               in0=pp[:].rearrange("p (c t) -> p c t", t=128),
                                                               in1=cmask[:].unsqueeze(1).to_broadcast([128, 4, 128]), op=ALU.mult),
                              r=[ppr], w=[scT.R(half)])

                def ph2(ch):
                    cs = slice(ch * 128, (ch + 1) * 128)
                    for hh in allh:
                        kb.op("act", lambda e, hh=hh: e.activation(out=Sbf.t[:, hh, :], in_=S.t[:, hh, :], func=AF.Copy,
                                                                   scale=ssc.t[:, hh, ch:ch + 1]),
                              r=[S.R(hh), ssc.R(hh)], w=[Sbf.R(hh)])
                    pO = [bank(), bank()]
                    for half, (pp, ppr) in enumerate(pO):
                        fns = []
                        for i in range(4):
                            hd = half * 4 + i
                            fns.append(lambda e, i=i, hd=hd: e.matmul(pp[:, i * 128:(i + 1) * 128], lhsT=vt.t[:, ch, hd, :],
                                                                      rhs=scT.t[:, hd, :], start=True, stop=False))
                            fns.append(lambda e, i=i, hd=hd: e.matmul(pp[:, i * 128:(i + 1) * 128], lhsT=Sbf.t[:, hd, :],
                                                                      rhs=qt.t[:, hd, cs], start=False, stop=True))
                        kb.grp("pe", fns, r=[vt.R(half * 4 + i) for i in range(4)] + [scT.R(half)] + [Sbf.R(half * 4 + i) for i in range(4)] +
                               [qt.R(half * 4 + i) for i in range(4)], w=[ppr])
                    pK = [bank(), bank()]
                    for half, (pp, ppr) in enumerate(pK):
                        kb.grp("pe", [lambda e, i=i: e.matmul(pp[:, i * 128:(i + 1) * 128], lhsT=kd.t[:, ch, half * 4 + i, :],
                                                              rhs=vt.t[:, ch, half * 4 + i, :], start=True, stop=True) for i in range(4)],
                               r=[kd.R(half * 4 + i) for i in range(4)] + [vt.R(half * 4 + i) for i in range(4)], w=[ppr])
                    for half, (pp, ppr) in enumerate(pK):
                        for i in range(4):
                            hh = half * 4 + i
                            kb.op("dve", lambda e, hh=hh, i=i: e.scalar_tensor_tensor(out=S.t[:, hh, :], in0=S.t[:, hh, :],
                                                                                     scalar=dec.t[:, hh, ch:ch + 1],
                                                                                     in1=pp[:, i * 128:(i + 1) * 128],
                                                                                     op0=ALU.mult, op1=ALU.add),
                                  r=[ppr, S.R(hh), dec.R(hh)], w=[S.R(hh)])
                    CH[ch] = pO

                def ph3(ch):
                    cs = slice(ch * 128, (ch + 1) * 128)
                    pO = CH[ch]
                    for half, (pp, ppr) in enumerate(pO):
                        kb.op("act", lambda e: e.activation(out=sqb[0].t[:, half * 4:half * 4 + 4, 0:128],
                                                            in_=pp[:].rearrange("p (c t) -> p c t", t=128), func=AF.Square),
                              r=[ppr], w=[sqb[0].R(half * 4 + i) for i in range(4)])
                    pS = [bank(), bank()]
                    for half, (pp, ppr) in enumerate(pS):
                        kb.grp("pe", [lambda e, i=i: e.matmul(pp[:, i * 128:(i + 1) * 128], lhsT=ones_h[:], rhs=sqb[0].t[:, half * 4 + i, 0:128],
                                                              start=True, stop=True) for i in range(4)], r=[sqb[0].R(half * 4 + i) for i in range(4)], w=[ppr])
                    for half, (pp, ppr) in enumerate(pS):
                        rs = rsf.next()
                        kb.op("act", lambda e: e.activation(out=rs.t[:], in_=pp[:], func=AF.Ln, bias=epsb[:, 0:1]), r=[ppr], w=[rs.R(0)])
                        kb.op("act", lambda e: e.activation(out=rs.t[:], in_=rs.t[:], func=AF.Exp, scale=-0.5), r=[rs.R(0)], w=[rs.R(0)])
                        po, por = pO[half]
                        kb.op("dve", lambda e: e.scalar_tensor_tensor(out=onf.t[:, half * 4:half * 4 + 4, :],
                                                                      in0=po[:].rearrange("p (c t) -> p c t", t=128),
                                                                      scalar=pcol(OFF_GN), in1=rs.t[:].rearrange("p (c t) -> p c t", t=128),
                                                                      op0=ALU.mult, op1=ALU.mult), r=[por, rs.R(0)], w=[onf.R(half)])
                    kb.op("dve", lambda e: e.tensor_tensor(out=xn.t[:, :, t * TT + ch * 128:t * TT + (ch + 1) * 128], in0=onf.t[:],
                                                            in1=sg.t[:, :, cs], op=ALU.mult),
                          r=[onf.R(0), onf.R(1)] + [sg.R(hh) for hh in allh], w=[xn.R(c, t) for c in range(KC)])


                ph1(0)
                for ch in range(4):
                    ph2(ch)
                    if ch + 1 < 4:
                        ph1(ch + 1)
                    ph3(ch)

            def posth(t):
                postnorm_t(pv, OFF_G + (6 + 3) * 8, t)
                if t == NT - 1:
                    for tt in range(NT):
                        pre_stats(tt)
                    PRE["stats"] = True
            out_proj(strA, lambda slot, k: slot.t[:, 0, k, :], KC, lambda k, t: xn.t[:, k, tsl(t)],
                     lambda t: [xn.R(k, t) for k in range(KC)], None, posth, tile_outer=True)

        ALIAS = {"toks": [], "bufs": []}

        def SB(tensor):
            b = Buf(tensor, init=ALIAS["toks"])
            ALIAS["bufs"].append(b)
            return b

        def stage_barrier():
            if ctx.plan:
                return
            best = {}
            for tok in ALIAS["toks"]:
                best[tok[0].name] = tok
            for b in ALIAS["bufs"]:
                for r in b.res.values():
                    for tok in ([r.w] if r.w is not None else []) + list(r.rs):
                        k = tok[0].name
                        if k not in best or best[k][1] < tok[1]:
                            best[k] = tok
            ALIAS["toks"] = list(best.values())
            ALIAS["bufs"] = []

        def run_all():
            for g in range(NG):
                if g == 0:
                    load_group(g)
                for l in range(2):
                    nst = 99 if DBG_STAGES is None else DBG_STAGES - 3 * l
                    if nst <= 0:
                        continue
                    with ExitStack() as st:
                        h = None
                        if not ctx.plan:
                            h = SB(st.enter_context(nc.sbuf_tensor(f"h_{g}_{l}_0", [128, JC, G], BF16)))
                        ffn(l, 0, h)
                        stage_barrier()
                    if nst <= 1:
                        continue
                    with ExitStack() as st:
                        if l == 0:
                            bufs = None
                            if not ctx.plan:
                                hc = SB(st.enter_context(nc.sbuf_tensor(f"hc_{g}", [128, KC, 32 + G], BF16)))
                                diag = [SB(st.enter_context(nc.sbuf_tensor(f"dg{i}_{g}", [128, CW, 128], BF16))) for i in range(2)]
                                hb = None
                                bufs = (hc, diag, hb)
                            conv_mixer(*(bufs if bufs else (None, None, None)))
                        else:
                            bufs = (None,) * 9
                            if not ctx.plan:
                                mk = lambda nm, shp, dty: SB(st.enter_context(nc.sbuf_tensor(f"{nm}_{g}", shp, dty)))
                                bufs = (mk("qt", [128, KC, TT], BF16), mk("kt", [128, KC, TT], BF16), mk("sg", [128, KC, TT], BF16),
                                        mk("kd", [128, 4, KC, 128], BF16), mk("vt", [128, 4, KC, 128], BF16),
                                        mk("scT", [128, KC, 128], BF16), mk("Sbf", [128, KC, 128], BF16),
                                        None, mk("onf", [128, KC, 128], F32))
                            hgrn_mixer(*bufs)
                        stage_barrier()
                    if nst <= 2:
                        continue
                    with ExitStack() as st:
                        h = None
                        if not ctx.plan:
                            h = SB(st.enter_context(nc.sbuf_tensor(f"h_{g}_{l}_1", [128, JC, G], BF16)))
                        ffn(l, 1, h)
                        stage_barrier()
                for tl in range(NT):
                    tbs = range(tl * (TT // 128), (tl + 1) * (TT // 128))
                    store_group(g, tbs)
                    if g + 1 < NG:
                        load_group(g + 1, tbs)
                        if not ctx.plan:
                            pre_stats(tl)
                            PRE["stats"] = True

        ctx.plan = True
        run_all()
        strA = WStream(kb, wA, planA, 2, issueA)
        strO = WStream(kb, wO, planO, 1, issueO)
        ctx.plan = False
        run_all()
        sp = kb.eng["sp"]
        for s in io + [q.slot for q in ystg]:
            if s.cnt:
                sp.obj.wait_ge(s.sem, s.cnt)
        kb.barrier(("pe", "act", "dve", "pool"))
        kb.counts = {k: v.count for k, v in kb.eng.items()}
        build_program.last_counts = kb.counts
    return nc


def host_inputs(x_seq, norm_gains, ffn_w_in, ffn_w_out, conv_w_in, conv_b_in, conv_w_dw, conv_b_dw, conv_ln_g, conv_ln_b,
                conv_w_out, conv_b_out, hgrn_w_in, hgrn_lb_logits, hgrn_g_norm, hgrn_w_out):
    f = lambda a: np.ascontiguousarray(np.asarray(a, dtype=np.float32))
    pvec = np.zeros((NPV, 128), np.float32)
    pvec[OFF_G:OFF_G + 96] = f(norm_gains).reshape(96, 128)
    pvec[OFF_CBIN:OFF_CBIN + 16] = f(conv_b_in).reshape(16, 128)
    pvec[OFF_CBDW:OFF_CBDW + 8] = f(conv_b_dw).reshape(8, 128)
    pvec[OFF_LNG:OFF_LNG + 8] = f(conv_ln_g).reshape(8, 128)
    pvec[OFF_LNB:OFF_LNB + 8] = f(conv_ln_b).reshape(8, 128)
    pvec[OFF_CBO:OFF_CBO + 8] = f(conv_b_out).reshape(8, 128)
    pvec[OFF_WDW:OFF_WDW + 248] = f(conv_w_dw).reshape(248, 128)
    pvec[OFF_LBL:OFF_LBL + 16] = f(hgrn_lb_logits).reshape(16, 128)
    pvec[OFF_GN:OFF_GN + 1] = f(hgrn_g_norm).reshape(1, 128)
    return {
        "x": f(x_seq), "ffn_w_in": f(ffn_w_in), "ffn_w_out": f(ffn_w_out), "conv_w_in": f(conv_w_in)[0],
        "conv_w_out": f(conv_w_out)[0], "hgrn_w_in": f(hgrn_w_in)[0], "hgrn_w_out": f(hgrn_w_out)[0],
        "pvec": pvec, "ident": np.eye(128, dtype=np.float32), "cmask": np.triu(np.ones((128, 128), np.float32)),
    }


_NC_CACHE = {}


def kernel(x, norm_gains, ffn_w_in, ffn_w_out, conv_w_in, conv_b_in, conv_w_dw, conv_b_dw, conv_ln_g, conv_ln_b,
           conv_w_out, conv_b_out, hgrn_w_in, hgrn_lb_logits, hgrn_g_norm, hgrn_w_out, _seq_t=None):
    x = np.asarray(x, dtype=np.float32)
    B, S_, _ = x.shape
    seq_t = S_ if _seq_t is None else _seq_t
    if seq_t not in _NC_CACHE:
        _NC_CACHE[seq_t] = build_program(seq_t)
    nc = _NC_CACHE[seq_t]
    in_maps = []
    for c in range(NCORES):
        in_maps.append(host_inputs(x[c % B, :seq_t], norm_gains, ffn_w_in, ffn_w_out, conv_w_in, conv_b_in, conv_w_dw, conv_b_dw,
                                   conv_ln_g, conv_ln_b, conv_w_out, conv_b_out, hgrn_w_in, hgrn_lb_logits, hgrn_g_norm, hgrn_w_out))
    res = run_bass_kernel_spmd(nc, in_maps, core_ids=list(range(NCORES)))
    out = np.stack([np.asarray(res.results[b]["y"], dtype=np.float32) for b in range(B)], axis=0)
    return out
```
